# Optimizing a Trainium2 kernel written in Bass

```python
import math
import jax, jax.numpy as jnp
from jax import lax
import numpy as np

D_MODEL = 2048
BATCH = 2
SEQ = 4096
DEPTH = 2

GRID_W = 64
CTX_LEN = 256
N_MIXERS = 2
N_NA_LAYERS = (DEPTH + N_MIXERS - 1) // N_MIXERS
N_S5_LAYERS = DEPTH // N_MIXERS
N_HEADS = 16
HEAD_DIM = D_MODEL // N_HEADS
WIN_R = 8
WIN_C = 16
Q_BLK_C = 16
K_BLK_C = 2 * WIN_C
SSM_GROUP = 16
N_GROUPS = D_MODEL // SSM_GROUP
SSM_STATE = 64
N_EXPERTS = 16
EXPERT_FF = D_MODEL
CAPACITY_FACTOR = 2
DN_ALPHA = (2 * DEPTH) ** 0.25
DN_BETA = (8 * DEPTH) ** -0.25
LN_EPS = 1e-5
NEG_INF = -1e30

kernel_name = 'hybrid_natten_s5_ecmoe_dit_block'


def layer_norm(x, g, b):
    xf = x.astype(jnp.float32)
    mu = jnp.mean(xf, -1, keepdims=True)
    var = jnp.mean(jnp.square(xf - mu), -1, keepdims=True)
    return ((xf - mu) * lax.rsqrt(var + LN_EPS) * g + b).astype(x.dtype)


def neighborhood_attention(q, k, v, k_ctx, v_ctx, rpb):
    B, L, H, dh = q.shape
    rows = L // GRID_W
    kr = min(WIN_R, rows)
    qg = q.reshape(B, rows, GRID_W, H, dh)
    kg = k.reshape(B, rows, GRID_W, H, dh)
    vg = v.reshape(B, rows, GRID_W, H, dh)
    r = jnp.arange(rows)
    rs = jnp.clip(r - kr // 2, 0, rows - kr)
    row_idx = rs[:, None] + jnp.arange(kr)[None, :]
    dr_idx = row_idx - r[:, None] + (WIN_R - 1)
    scale = HEAD_DIM ** -0.5
    n_loc = kr * K_BLK_C
    outs = []
    for j in range(GRID_W // Q_BLK_C):
        c0 = j * Q_BLK_C
        cs = min(max(c0 - WIN_C // 2, 0), GRID_W - K_BLK_C)
        qcol = c0 + jnp.arange(Q_BLK_C)
        kcol = cs + jnp.arange(K_BLK_C)
        cstart = jnp.clip(qcol - WIN_C // 2, 0, GRID_W - WIN_C)
        valid = (kcol[None] >= cstart[:, None]) & (kcol[None] < cstart[:, None] + WIN_C)
        dc_idx = jnp.clip(kcol[None] - qcol[:, None] + WIN_C - 1, 0, 2 * WIN_C - 2)
        bias = rpb[:, dr_idx[:, None, :, None], dc_idx[None, :, None, :]]
        qb = qg[:, :, c0:c0 + Q_BLK_C]
        kb = kg[:, row_idx, cs:cs + K_BLK_C]
        vb = vg[:, row_idx, cs:cs + K_BLK_C]
        s_loc = jnp.einsum('brqhd,brkwhd->bhrqkw', qb, kb).astype(jnp.float32) * scale
        s_loc = s_loc + bias.astype(jnp.float32)[None]
        s_loc = jnp.where(valid[:, None, :], s_loc, NEG_INF).reshape(B, H, rows, Q_BLK_C, n_loc)
        s_ctx = jnp.einsum('brqhd,bchd->bhrqc', qb, k_ctx).astype(jnp.float32) * scale
        p = jax.nn.softmax(jnp.concatenate([s_loc, s_ctx], axis=-1), axis=-1).astype(v.dtype)
        p_loc = p[..., :n_loc].reshape(B, H, rows, Q_BLK_C, kr, K_BLK_C)
        o = (jnp.einsum('bhrqkw,brkwhd->brqhd', p_loc, vb)
             + jnp.einsum('bhrqc,bchd->brqhd', p[..., n_loc:], v_ctx))
        outs.append(o)
    return jnp.concatenate(outs, axis=2).reshape(B, L, H * dh)


def context_attention(q, k, v):
    s = jnp.einsum('bqhd,bkhd->bhqk', q, k).astype(jnp.float32) * (HEAD_DIM ** -0.5)
    p = jax.nn.softmax(s, axis=-1).astype(v.dtype)
    o = jnp.einsum('bhqk,bkhd->bqhd', p, v)
    return o.reshape(q.shape[0], q.shape[1], D_MODEL)


def na_mixer(h_lat, h_ctx, w_qkv, w_o, rpb, need_ctx):
    B, L, _ = h_lat.shape
    Lc = h_ctx.shape[1]
    ql, kl, vl = jnp.split((h_lat @ w_qkv).reshape(B, L, 3 * N_HEADS, HEAD_DIM), 3, axis=2)
    qc, kc, vc = jnp.split((h_ctx @ w_qkv).reshape(B, Lc, 3 * N_HEADS, HEAD_DIM), 3, axis=2)
    o_lat = neighborhood_attention(ql, kl, vl, kc, vc, rpb) @ w_o
    o_ctx = context_attention(qc, kc, vc) @ w_o if need_ctx else None
    return o_lat, o_ctx


def s5_discretize(lam_re, lam_im, log_step, b_re, b_im):
    lr = jnp.minimum(lam_re.astype(jnp.float32), -1e-4)
    li = lam_im.astype(jnp.float32)
    dt = jnp.exp(log_step.astype(jnp.float32))[:, None]
    mag = jnp.exp(lr * dt)
    ar = mag * jnp.cos(li * dt)
    ai = mag * jnp.sin(li * dt)
    nr = ar - 1.0
    den = lr * lr + li * li
    cr = ((nr * lr + ai * li) / den)[..., None]
    ci = ((ai * lr - nr * li) / den)[..., None]
    br = cr * b_re - ci * b_im
    bi = cr * b_im + ci * b_re
    return ar, ai, br, bi


def _scan_op(e1, e2):
    a1r, a1i, b1r, b1i = e1
    a2r, a2i, b2r, b2i = e2
    return (a2r * a1r - a2i * a1i, a2r * a1i + a2i * a1r,
            a2r * b1r - a2i * b1i + b2r, a2r * b1i + a2i * b1r + b2i)


def s5_scan(ar, ai, bur, bui, h0r, h0i, reverse):
    first = bur.shape[1] - 1 if reverse else 0
    bur = bur.at[:, first].add(ar * h0r - ai * h0i)
    bui = bui.at[:, first].add(ar * h0i + ai * h0r)

    def one(br, bi):
        a_r = jnp.broadcast_to(ar, br.shape)
        a_i = jnp.broadcast_to(ai, br.shape)
        _, _, xr, xi = lax.associative_scan(_scan_op, (a_r, a_i, br, bi), reverse=reverse, axis=0)
        return xr, xi

    return jax.vmap(one)(bur, bui)


def s5_readout(xr, xi, c_re, c_im):
    return jnp.einsum('bngp,ghp->bngh', xr, c_re) - jnp.einsum('bngp,ghp->bngh', xi, c_im)


def s5_mixer(h_lat, h_ctx, lam_re, lam_im, log_step, b_re, b_im, c_re, c_im, d_skip,
             w_val, w_gate, need_ctx):
    B, n, _ = h_lat.shape
    nc = h_ctx.shape[1]
    u_lat = h_lat.reshape(B, n, N_GROUPS, SSM_GROUP)
    u_ctx = h_ctx.reshape(B, nc, N_GROUPS, SSM_GROUP)
    d = d_skip.reshape(N_GROUPS, SSM_GROUP)
    y_lat = d * u_lat
    y_ctx = d * u_ctx if need_ctx else None
    zero = jnp.zeros((B, N_GROUPS, SSM_STATE), jnp.float32)
    for direction in range(2):
        rev = direction == 1
        ar, ai, br, bi = s5_discretize(lam_re[direction], lam_im[direction], log_step[direction],
                                       b_re[direction], b_im[direction])
        cbr = jnp.einsum('bngh,gph->bngp', u_ctx, br)
        cbi = jnp.einsum('bngh,gph->bngp', u_ctx, bi)
        xcr, xci = s5_scan(ar, ai, cbr, cbi, zero, zero, rev)
        last = 0 if rev else nc - 1
        lbr = jnp.einsum('bngh,gph->bngp', u_lat, br)
        lbi = jnp.einsum('bngh,gph->bngp', u_lat, bi)
        xlr, xli = s5_scan(ar, ai, lbr, lbi, xcr[:, last], xci[:, last], rev)
        y_lat = y_lat + s5_readout(xlr, xli, c_re[direction], c_im[direction])
        if need_ctx:
            y_ctx = y_ctx + s5_readout(xcr, xci, c_re[direction], c_im[direction])

    def glu(y):
        z = jax.nn.gelu(y.reshape(B, -1, D_MODEL)).astype(h_lat.dtype)
        return (z @ w_val) * jax.nn.sigmoid(z @ w_gate)

    return glu(y_lat), (glu(y_ctx) if need_ctx else None)


def expert_choice_ffn(h, w_router, w_gate, w_up, w_down):
    B, n, D = h.shape
    cap = CAPACITY_FACTOR * n // N_EXPERTS
    logits = jnp.einsum('bnd,de->ben', h, w_router).astype(jnp.float32)
    aff = jax.nn.softmax(logits, axis=1)
    gate, idx = lax.top_k(aff, cap)
    xe = jax.vmap(lambda hb, ib: hb[ib])(h, idx)
    a = jnp.einsum('becd,edf->becf', xe, w_gate)
    u = jnp.einsum('becd,edf->becf', xe, w_up)
    ye = jnp.einsum('becf,efd->becd', jax.nn.silu(a) * u, w_down) * gate[..., None].astype(h.dtype)
    return jax.vmap(lambda ib, yb: jnp.zeros((n, D), yb.dtype).at[ib.reshape(-1)].add(
        yb.reshape(-1, D)))(idx, ye)


def setup_inputs(seed: int = 0) -> dict:
    key = jax.random.key(seed)
    ks = jax.random.split(key, 28)
    f32 = jnp.float32
    D = D_MODEL

    def nrm(k, shape, std):
        return jax.random.normal(k, shape, f32) * std

    s5_shape = (N_S5_LAYERS, 2, N_GROUPS, SSM_STATE)
    return {
        'x': nrm(ks[0], (BATCH, SEQ, D), 1.0),
        'c': nrm(ks[1], (BATCH, D), 1.0),
        'ctx': nrm(ks[2], (BATCH, CTX_LEN, D), 1.0),
        'c_ctx': nrm(ks[3], (D,), 1.0),
        'w_mod': nrm(ks[4], (DEPTH, D, 6 * D), 0.5 * D ** -0.5),
        'b_mod': nrm(ks[5], (DEPTH, 6 * D), 0.02),
        'ln_g': 1.0 + nrm(ks[6], (DEPTH, 2, D), 0.02),
        'ln_b': nrm(ks[7], (DEPTH, 2, D), 0.02),
        'na_w_qkv': nrm(ks[8], (N_NA_LAYERS, D, 3 * D), D ** -0.5),
        'na_w_o': nrm(ks[9], (N_NA_LAYERS, D, D), DN_BETA * D ** -0.5),
        'na_rpb': nrm(ks[10], (N_NA_LAYERS, N_HEADS, 2 * WIN_R - 1, 2 * WIN_C - 1), 0.02),
        's5_lam_re': -0.5 + nrm(ks[11], s5_shape, 0.01),
        's5_lam_im': math.pi * jnp.arange(SSM_STATE, dtype=f32) + nrm(ks[12], s5_shape, 0.01),
        's5_log_step': jax.random.uniform(ks[13], (N_S5_LAYERS, 2, N_GROUPS), f32,
                                          math.log(1e-3), math.log(1e-1)),
        's5_b_re': nrm(ks[14], s5_shape + (SSM_GROUP,), (2 * SSM_GROUP) ** -0.5),
        's5_b_im': nrm(ks[15], s5_shape + (SSM_GROUP,), (2 * SSM_GROUP) ** -0.5),
        's5_c_re': nrm(ks[16], (N_S5_LAYERS, 2, N_GROUPS, SSM_GROUP, SSM_STATE), SSM_STATE ** -0.5),
        's5_c_im': nrm(ks[17], (N_S5_LAYERS, 2, N_GROUPS, SSM_GROUP, SSM_STATE), SSM_STATE ** -0.5),
        's5_d': nrm(ks[18], (N_S5_LAYERS, D), 1.0),
        's5_w_val': nrm(ks[19], (N_S5_LAYERS, D, D), DN_BETA * D ** -0.5),
        's5_w_gate': nrm(ks[20], (N_S5_LAYERS, D, D), D ** -0.5),
        'moe_w_router': nrm(ks[21], (DEPTH, D, N_EXPERTS), D ** -0.5),
        'moe_w_gate': nrm(ks[22], (DEPTH, N_EXPERTS, D, EXPERT_FF), D ** -0.5),
        'moe_w_up': nrm(ks[23], (DEPTH, N_EXPERTS, D, EXPERT_FF), D ** -0.5),
        'moe_w_down': nrm(ks[24], (DEPTH, N_EXPERTS, EXPERT_FF, D), DN_BETA * EXPERT_FF ** -0.5),
    }


def reference(x, c, ctx, c_ctx, w_mod, b_mod, ln_g, ln_b, na_w_qkv, na_w_o, na_rpb,
              s5_lam_re, s5_lam_im, s5_log_step, s5_b_re, s5_b_im, s5_c_re, s5_c_im, s5_d,
              s5_w_val, s5_w_gate, moe_w_router, moe_w_gate, moe_w_up, moe_w_down):
    x_lat = x
    x_ctx = ctx
    for i in range(DEPTH):
        need_ctx = i < DEPTH - 1
        m_lat = (jax.nn.silu(c) @ w_mod[i] + b_mod[i])[:, None, :]
        m_ctx = (jax.nn.silu(c_ctx) @ w_mod[i] + b_mod[i])[None, None, :]
        sh1, sc1, g1, sh2, sc2, g2 = jnp.split(m_lat, 6, axis=-1)
        csh1, csc1, cg1, csh2, csc2, cg2 = jnp.split(m_ctx, 6, axis=-1)
        h_lat = x_lat * (1.0 + sc1) + sh1
        h_ctx = x_ctx * (1.0 + csc1) + csh1
        j = i // N_MIXERS
        if i % N_MIXERS == 0:
            o_lat, o_ctx = na_mixer(h_lat, h_ctx, na_w_qkv[j], na_w_o[j], na_rpb[j], need_ctx)
        else:
            o_lat, o_ctx = s5_mixer(h_lat, h_ctx, s5_lam_re[j], s5_lam_im[j], s5_log_step[j],
                                    s5_b_re[j], s5_b_im[j], s5_c_re[j], s5_c_im[j], s5_d[j],
                                    s5_w_val[j], s5_w_gate[j], need_ctx)
        x_lat = layer_norm(DN_ALPHA * x_lat + g1 * o_lat, ln_g[i, 0], ln_b[i, 0])
        h2 = x_lat * (1.0 + sc2) + sh2
        f_lat = expert_choice_ffn(h2, moe_w_router[i], moe_w_gate[i], moe_w_up[i], moe_w_down[i])
        x_lat = layer_norm(DN_ALPHA * x_lat + g2 * f_lat, ln_g[i, 1], ln_b[i, 1])
        if need_ctx:
            x_ctx = layer_norm(DN_ALPHA * x_ctx + cg1 * o_ctx, ln_g[i, 0], ln_b[i, 0])
            hc2 = x_ctx * (1.0 + csc2) + csh2
            f_ctx = expert_choice_ffn(hc2, moe_w_router[i], moe_w_gate[i], moe_w_up[i], moe_w_down[i])
            x_ctx = layer_norm(DN_ALPHA * x_ctx + cg2 * f_ctx, ln_g[i, 1], ln_b[i, 1])
    return x_lat
```

```python
import contextlib
import math
import numpy as np
import concourse.bass as bass
import concourse.mybir as mybir
from concourse.bass_utils import run_bass_kernel_spmd

F32 = mybir.dt.float32
BF16 = mybir.dt.bfloat16
I32 = mybir.dt.int32
U32 = mybir.dt.uint32
AF = mybir.ActivationFunctionType
ALU = mybir.AluOpType
AX = mybir.AxisListType

NCORES = 8
D = 2048
KC = D // 128
B = 2
SEQ = 4096
CTX = 256
NH = 16
DH = 128
NE = 16
DEPTH = 2
DN_ALPHA = (2 * DEPTH) ** 0.25
LN_EPS = 1e-5


class Buf:
    __slots__ = ("name", "w", "r", "sem", "cnt", "excl")

    def __init__(self, name, excl=False):
        self.name = name
        self.w = None
        self.r = {}
        self.sem = None
        self.cnt = 0
        self.excl = excl


class Sched:
    EPOCH = 2000
    DMA_LIMIT = 1920

    def __init__(self, nc, stack):
        self.nc = nc
        self.stack = stack
        self.engs = {"pe": nc.tensor, "dve": nc.vector, "act": nc.scalar, "pool": nc.gpsimd, "sp": nc.sync}
        self.esem = {}
        self.ecnt = {}
        self.sems = {}
        self.nsem = 0
        self.seen = {e: {} for e in self.engs}
        self.eng_fin = {}
        self.dma_fin = {}
        self.free_dma = []
        for e in self.engs:
            self._new_esem(e)

    def _alloc(self, name):
        self.nsem += 1
        h = self.stack.enter_context(self.nc.semaphore(f"{name}_{self.nsem}"))
        key = self.nsem
        self.sems[key] = h
        return key

    def _new_esem(self, e):
        if e in self.esem and self.ecnt[e] > 0:
            self.eng_fin[self.esem[e]] = self.ecnt[e]
        self.esem[e] = self._alloc("e" + e)
        self.ecnt[e] = 0

    def _wait(self, e, deps):
        best = {}
        for k, c in deps:
            if c > best.get(k, 0):
                best[k] = c
        for k, c in best.items():
            if self.seen[e].get(k, 0) >= c:
                continue
            if e == "pe" and k == self.esem["pe"]:
                continue
            self.engs[e].wait_ge(self.sems[k], c)
            self.seen[e][k] = c

    @staticmethod
    def _deps(reads, writes):
        deps = []
        for b in reads:
            if b.w is not None:
                deps.append(b.w)
            if b.excl:
                deps.extend(b.r.items())
        for b in writes:
            if b.w is not None:
                deps.append(b.w)
            deps.extend(b.r.items())
        return deps

    @staticmethod
    def _record(ev, reads, writes):
        k, c = ev
        for b in reads:
            if b.excl:
                b.w = ev
                b.r = {}
            else:
                b.r[k] = max(b.r.get(k, 0), c)
        for b in writes:
            b.w = ev
            b.r = {}

    def op(self, e, fn, reads=(), writes=()):
        if self.ecnt[e] >= self.EPOCH:
            self._new_esem(e)
        self._wait(e, self._deps(reads, writes))
        ins = fn()
        self.ecnt[e] += 1
        ins.then_inc(self.sems[self.esem[e]], 1)
        ev = (self.esem[e], self.ecnt[e])
        self._record(ev, reads, writes)
        return ins

    def _dma_sem(self, owner):
        if owner.sem is not None and owner.cnt < self.DMA_LIMIT:
            return
        if self.free_dma:
            owner.sem, owner.cnt = self.free_dma.pop()
        else:
            owner.sem = self._alloc("d")
            owner.cnt = 0

    def release(self, bufs):
        for b in bufs:
            if b.sem is not None:
                if b.cnt < self.DMA_LIMIT:
                    self.free_dma.append((b.sem, b.cnt))
                b.sem = None

    def dma(self, q, out, in_, reads=(), writes=(), owner=None, **kw):
        if owner is None:
            owner = (list(writes) + list(reads))[0]
        self._dma_sem(owner)
        self._wait(q, self._deps(reads, writes))
        ins = self.engs[q].dma_start(out=out, in_=in_, **kw)
        owner.cnt += 16
        ins.then_inc(self.sems[owner.sem], 16)
        ev = (owner.sem, owner.cnt)
        self._record(ev, reads, writes)
        self.dma_fin[owner.sem] = owner.cnt
        return ins

    def idma(self, out, out_offset, in_, in_offset, reads=(), writes=(), owner=None, **kw):
        if owner is None:
            owner = (list(writes) + list(reads))[0]
        self._dma_sem(owner)
        self._wait("pool", self._deps(reads, writes))
        ins = self.engs["pool"].indirect_dma_start(out=out, out_offset=out_offset, in_=in_, in_offset=in_offset, **kw)
        owner.cnt += 16
        ins.then_inc(self.sems[owner.sem], 16)
        self._record((owner.sem, owner.cnt), reads, writes)
        self.dma_fin[owner.sem] = owner.cnt
        return ins

    def coll(self, kind, op, in_ap, out_ap):
        if not hasattr(self, "ccbuf"):
            self.ccbuf = Buf("cc")
            self.ccbuf.sem = self._alloc("cc")
        b = self.ccbuf
        ins = self.engs["pool"].collective_compute(kind, op, replica_groups=[list(range(NCORES))],
                                                   ins=[in_ap], outs=[out_ap])
        b.cnt += 1
        ins.then_inc(self.sems[b.sem], 1)
        self.dma_fin[b.sem] = b.cnt
        return ins

    def _all_deps(self):
        deps = list(self.dma_fin.items())
        deps.extend(self.eng_fin.items())
        for e in self.engs:
            if self.ecnt[e] > 0:
                deps.append((self.esem[e], self.ecnt[e]))
        return deps

    def barrier(self):
        deps = self._all_deps()
        for e in self.engs:
            self._wait(e, [d for d in deps if d[0] != self.esem[e]])

    def finish(self):
        self._wait("sp", self._all_deps())


class Ctx:
    def __init__(self):
        self.nc = bass.Bass("TRN2", target_bir_lowering=False)
        self.stack = contextlib.ExitStack()
        self.S = Sched(self.nc, self.stack)
        self.n = 0
        self.live = []

    def dram_in(self, name, shape, dt=F32):
        return self.nc.dram_tensor(name, list(shape), dt, kind="ExternalInput").ap()

    def dram_out(self, name, shape, dt=F32):
        return self.nc.dram_tensor(name, list(shape), dt, kind="ExternalOutput").ap()

    def dram_tmp(self, name, shape, dt=F32):
        return self.nc.dram_tensor(name, list(shape), dt, kind="Internal").ap()

    def sb(self, name, shape, dt=F32):
        self.n += 1
        name = f"sb{self.n}_{name}"
        t = self.stack.enter_context(self.nc.sbuf_tensor(name, list(shape), dt))
        b = Buf(name)
        self.live.append(b)
        return t, b

    def psum_banks(self, n=8):
        out = []
        for i in range(n):
            t = self.stack.enter_context(self.nc.psum_tensor(f"ps{i}", [128, 512], F32))
            out.append((t, Buf(f"ps{i}", excl=True)))
        return out

    @contextlib.contextmanager
    def scope(self):
        outer = self.stack
        outer_live = self.live
        with contextlib.ExitStack() as sub:
            self.stack = sub
            self.live = []
            try:
                yield
            finally:
                self.S.barrier()
                self.S.release(self.live)
                self.stack = outer
                self.live = outer_live

    def close(self):
        self.S.finish()
        self.stack.close()
        return self.nc


def run_spmd(nc, in_maps):
    res = run_bass_kernel_spmd(nc, in_maps, core_ids=list(range(NCORES)))
    return res.results


OWN0 = 1088
OWN1 = 1024
NQKV = 1792
MODN = 12 * 256
COL_CS = (0, 8, 24, 32)
ATT_SCALE = DH ** -0.5


def tok_blocks(n0, n1, blk=512):
    return [(t0, min(blk, n1 - t0)) for t0 in range(n0, n1, blk)]


def tok_tiles(nt):
    return [(t0, min(128, nt - t0)) for t0 in range(0, nt, 128)]


class Mod:
    def __init__(self, C, m_all, bsel_t, bsel_b):
        self.C = C
        self.m = m_all.rearrange("(c r) (v n) -> c r v n", r=3, n=256)
        self.bsel_t, self.bsel_b = bsel_t, bsel_b

    def rep_raw(self, q, tile, buf, lv, r):
        src = self.m[:, r, lv, :].partition_broadcast(128)
        self.C.S.dma(q, tile[:].rearrange("p (c n) -> p c n", c=8), src, writes=[buf])

    def rep(self, tile, buf, tmp, tmpb, lv, ctx):
        C = self.C
        nc, S = C.nc, C.S
        if ctx:
            self.rep_raw("sp", tile, buf, lv, 2)
            return
        self.rep_raw("sp", tile, buf, lv, 0)
        self.rep_raw("sp", tmp, tmpb, lv, 1)
        bs, bsb = self.bsel_t, self.bsel_b
        S.op("dve", lambda: nc.vector.tensor_scalar(tile[:], tile[:], bs[:, 0:1], None, ALU.mult),
             reads=[buf, bsb], writes=[buf])
        S.op("dve", lambda: nc.vector.scalar_tensor_tensor(tile[:], tmp[:], bs[:, 1:2], tile[:], ALU.mult, ALU.add),
             reads=[buf, tmpb, bsb], writes=[buf])

    def fm_raw(self, q, out_ap, buf, lv, r):
        src = self.m[:, r, lv, :].rearrange("c (h p) -> p c h", p=128)
        ov = out_ap.rearrange("p (c h) -> p c h", h=2)
        for h in range(2):
            self.C.S.dma(q, ov[:, :, h], src[:, :, h], writes=[buf], allow_slow_non_contiguous=True)

    def fm(self, tile, buf, col, lv, ctx):
        C = self.C
        nc, S = C.nc, C.S
        o = tile[:, col * KC:(col + 1) * KC]
        if ctx:
            self.fm_raw("sp", o, buf, lv, 2)
            return
        t = tile[:, (col + 1) * KC:(col + 2) * KC]
        self.fm_raw("sp", o, buf, lv, 0)
        self.fm_raw("sp", t, buf, lv, 1)
        bs, bsb = self.bsel_t, self.bsel_b
        S.op("dve", lambda: nc.vector.tensor_scalar(o, o, bs[:, 0:1], None, ALU.mult), reads=[buf, bsb], writes=[buf])
        S.op("dve", lambda: nc.vector.scalar_tensor_tensor(o, t, bs[:, 1:2], o, ALU.mult, ALU.add),
             reads=[buf, bsb], writes=[buf])


def emit_mod(C, ps, cT, wmod, bmod, m_loc, m_all, cc=True):
    nc, S = C.nc, C.S
    with C.scope():
        ct, ctb = C.sb("ct", [128, KC * 3])
        st, stb = C.sb("st", [128, KC * 3])
        wts = [C.sb(f"mw{i}", [128, KC, 512]) for i in range(2)]
        bt, btb = C.sb("mbt", [3, MODN])
        ot, otb = C.sb("mot", [3, MODN])
        S.dma("sp", ct[:], cT, writes=[ctb])
        S.dma("sp", bt[:], bmod, writes=[btb])
        S.op("act", lambda: nc.scalar.activation(out=st[:], in_=ct[:], func=AF.Silu), reads=[ctb], writes=[stb])
        wv = wmod.rearrange("(kc p) n -> p kc n", p=128)
        for nb in range(MODN // 512):
            wt, wb = wts[nb % 2]
            sl = slice(nb * 512, (nb + 1) * 512)
            S.dma("sp", wt[:], wv[:, :, sl], writes=[wb])
            pt, pb = ps[nb % 2]
            for kc in range(KC):
                S.op("pe", lambda: nc.tensor.matmul(pt[0:3, :], lhsT=st[:, kc * 3:(kc + 1) * 3], rhs=wt[:, kc, :],
                                                    start=(kc == 0), stop=(kc == KC - 1)), reads=[stb, wb], writes=[pb])
            S.op("dve", lambda: nc.vector.tensor_tensor(ot[:, sl], pt[0:3, :], bt[:, sl], ALU.add),
                 reads=[pb, btb], writes=[otb])
        S.dma("sp", m_loc, ot[:], reads=[otb])
        S.barrier()
        if cc:
            S.coll("AllGather", ALU.bypass, m_loc, m_all)


def emit_qkv(C, ps, M, xT, wqkv, qT_s, kT_s, v_s):
    nc, S = C.nc, C.S
    with C.scope():
        xt, xb = C.sb("xt", [128, KC, NQKV])
        mv, mvb = C.sb("mv", [128, 6 * KC])
        S.dma("sp", xt[:].rearrange("p k t -> p (k t)"), xT, writes=[xb])
        M.fm(mv, mvb, 0, 1, False)
        M.fm(mv, mvb, 2, 0, False)
        M.fm(mv, mvb, 4, 1, True)
        M.fm(mv, mvb, 5, 0, True)
        for c0 in (0, 4 * KC):
            S.op("dve", lambda: nc.vector.tensor_scalar(mv[:, c0:c0 + KC], mv[:, c0:c0 + KC], 1.0, None, ALU.add),
                 reads=[mvb], writes=[mvb])
        for kc in range(KC):
            for (a, b_, csc, csh) in ((0, 1536, 0, 2 * KC), (1536, NQKV, 4 * KC, 5 * KC)):
                S.op("dve", lambda: nc.vector.tensor_scalar(xt[:, kc, a:b_], xt[:, kc, a:b_], mv[:, csc + kc:csc + kc + 1],
                                                            mv[:, csh + kc:csh + kc + 1], ALU.mult, ALU.add),
                     reads=[xb, mvb], writes=[xb])
        wv = wqkv.rearrange("(kc p) n -> p kc n", p=128)
        wts = [C.sb(f"qw{i}", [128, KC, 256]) for i in range(2)]
        ots = [C.sb(f"qo{i}", [128, NQKV]) for i in range(2)]
        vts = [C.sb(f"qv{i}", [128, 256]) for i in range(3)]
        pi = 0
        oi = 0
        vi = 0
        qblocks = tok_blocks(256, 1280) + [(1536, 64)]
        kblocks = tok_blocks(0, NQKV)
        for nb in range(3 * D // 256):
            wt, wb = wts[nb % 2]
            S.dma("sp", wt[:], wv[:, :, nb * 256:(nb + 1) * 256], writes=[wb])
            if nb < 16:
                isq = nb < 8
                for nn in range(2):
                    ot, ob = ots[oi % 2]
                    oi += 1
                    off = 0
                    for (t0, tn) in (qblocks if isq else kblocks):
                        pt, pb = ps[pi % 8]
                        pi += 1
                        for kc in range(KC):
                            S.op("pe", lambda: nc.tensor.matmul(pt[:, 0:tn], lhsT=wt[:, kc, nn * 128:(nn + 1) * 128],
                                                                rhs=xt[:, kc, t0:t0 + tn], start=(kc == 0), stop=(kc == KC - 1)),
                                 reads=[wb, xb], writes=[pb])
                        if isq and tn == 512:
                            rb = off // 512
                            S.op("act", lambda: nc.scalar.copy(
                                out=ot[:, 0:1024].rearrange("p (j r c) -> p j r c", j=4, c=16)[:, :, 8 * rb:8 * rb + 8, :],
                                in_=pt[:, 0:512].rearrange("p (r j c) -> p j r c", j=4, c=16)), reads=[pb], writes=[ob])
                        else:
                            S.op("act", lambda: nc.scalar.copy(out=ot[:, off:off + tn], in_=pt[:, 0:tn]), reads=[pb], writes=[ob])
                        off += tn
                    n0 = (nb % 8) * 256 + nn * 128
                    if isq:
                        S.dma("pool", qT_s[n0:n0 + 128, :], ot[:, 0:OWN0], reads=[ob])
                    else:
                        S.dma("pool", kT_s[n0:n0 + 128, :], ot[:, 0:NQKV], reads=[ob])
            else:
                n0 = (nb - 16) * 256
                for tile in range(14):
                    pt, pb = ps[pi % 8]
                    pi += 1
                    for kc in range(KC):
                        S.op("pe", lambda: nc.tensor.matmul(pt[:, 0:256], lhsT=xt[:, kc, tile * 128:(tile + 1) * 128], rhs=wt[:, kc, :],
                                                            start=(kc == 0), stop=(kc == KC - 1)), reads=[wb, xb], writes=[pb])
                    vt, vb = vts[vi % 3]
                    vi += 1
                    S.op("act", lambda: nc.scalar.copy(out=vt[:], in_=pt[:, 0:256]), reads=[pb], writes=[vb])
                    S.dma("pool", v_s[tile * 128:(tile + 1) * 128, n0:n0 + 256], vt[:], reads=[vb])


def emit_attn(C, ps, qT_s, kT_s, v_s, bias_tab, mask_tab, ones_d, aT_s, nh=NH, lvl=9):
    nc, S = C.nc, C.S
    RSLOT = (0, 1, 1, 2)
    JCLS = (0, 1, 1, 2)
    with C.scope():
        mk, mkb = C.sb("mk", [128, 27 * 64])
        on, onb = C.sb("on", [128, 128])
        S.dma("sp", mk[:], mask_tab, writes=[mkb])
        S.dma("sp", on[:], ones_d, writes=[onb])
        qts = [C.sb(f"aq{i}", [128, OWN0]) for i in range(2)]
        kbs = [C.sb(f"ak{i}", [128, 4, 768]) for i in range(2)]
        kcs = [C.sb(f"ac{i}", [128, 256]) for i in range(2)]
        vts = [C.sb(f"av{i}", [128, 26, 128]) for i in range(2)]
        bis = [C.sb(f"ab{i}", [128, 9 * 64]) for i in range(2)]
        bms = [C.sb(f"am{i}", [128, 27 * 64]) for i in range(2)]
        ots = [C.sb(f"ao{i}", [128, OWN0]) for i in range(2)]
        sls = [C.sb(f"as{i}", [128, 768]) for i in range(2)]
        pls = [C.sb(f"ap{i}", [128, 768]) for i in range(2)]
        pcs = [C.sb(f"apc{i}", [128, 512]) for i in range(2)]
        rcs = [C.sb(f"ar{i}", [128, 256]) for i in range(2)]
        it = 0
        for h in range(nh):
            qt, qb = qts[h % 2]
            kb, kbb = kbs[h % 2]
            kc_, kcb = kcs[h % 2]
            vt, vb = vts[h % 2]
            bi, bib = bis[h % 2]
            bm, bmb = bms[h % 2]
            ot, ob = ots[h % 2]
            hs = slice(h * 128, (h + 1) * 128)
            S.dma("sp", qt[:], qT_s[hs, :], writes=[qb])
            kband = kT_s[hs, 0:1536].rearrange("p (r c) -> p r c", c=64)
            for j in range(4):
                S.dma("sp", kb[:, j, :].rearrange("p (r c) -> p r c", c=32), kband[:, :, COL_CS[j]:COL_CS[j] + 32],
                      writes=[kbb])
            S.dma("sp", kc_[:], kT_s[hs, 1536:NQKV], writes=[kcb])
            vband = v_s[0:1536, hs].rearrange("(r c) d -> r c d", c=64)
            for j in range(4):
                for rt in range(6):
                    S.dma("sp", vt[:, j * 6 + rt, :], vband[4 * rt:4 * rt + 4, COL_CS[j]:COL_CS[j] + 32, :], writes=[vb])
            S.dma("sp", vt[:, 24:26, :], v_s[1536:NQKV, hs].rearrange("(t p) d -> p t d", p=128), writes=[vb])
            S.dma("sp", bi[:], bias_tab[h], writes=[bib])
            for rs in range(3):
                S.op("pool", lambda: nc.gpsimd.tensor_tensor(bm[:, rs * 576:(rs + 1) * 576], mk[:, rs * 576:(rs + 1) * 576],
                                                             bi[:], ALU.add), reads=[mkb, bib], writes=[bmb])
            ov = ot[:, 0:1024].rearrange("p (r c) -> p r c", c=64)
            if lvl == 1:
                S.op("dve", lambda: nc.vector.tensor_copy(ot[:, 0:576], bm[:, 0:576]), reads=[bmb, vb, kbb, kcb, qb], writes=[ob])
                S.dma("pool", aT_s[hs, :], ot[:], reads=[ob])
                continue
            for j in range(4):
                pa, pab = ps[(it % 2) * 3 + 0]
                pB, pbb = ps[(it % 2) * 3 + 1]
                pC, pcb = ps[(it % 2) * 3 + 2]
                pd, pdb = ps[6]
                pe_, peb = ps[7]
                sl, slb = sls[it % 2]
                pl, plb = pls[it % 2]
                pc, pcb2 = pcs[it % 2]
                rc, rcb = rcs[it % 2]
                it += 1
                jc = JCLS[j]
                qall = qt[:, j * 256:(j + 1) * 256]
                for t in range(2):
                    S.op("pe", lambda: nc.tensor.matmul(pa[:, t * 256:(t + 1) * 256], lhsT=kc_[:, t * 128:(t + 1) * 128],
                                                        rhs=qall, start=True, stop=True), reads=[kcb, qb], writes=[pab])
                for rg in range(4):
                    pt_, ptb = (pB, pbb) if rg < 2 else (pC, pcb)
                    for t in range(3):
                        col = ((rg % 2) * 3 + t) * 64
                        S.op("pe", lambda: nc.tensor.matmul(pt_[:, col:col + 64], lhsT=kb[:, j, (rg + t) * 128:(rg + t + 1) * 128],
                                                            rhs=qt[:, j * 256 + rg * 64:j * 256 + (rg + 1) * 64], start=True, stop=True),
                             reads=[kbb, qb], writes=[ptb])
                for rg in range(4):
                    pt_, ptb = (pB, pbb) if rg < 2 else (pC, pcb)
                    c0 = (rg % 2) * 192
                    b0 = (RSLOT[rg] * 3 + jc) * 192
                    S.op("dve", lambda: nc.vector.scalar_tensor_tensor(sl[:, rg * 192:(rg + 1) * 192], pt_[:, c0:c0 + 192], ATT_SCALE,
                                                                       bm[:, b0:b0 + 192], ALU.mult, ALU.add),
                         reads=[ptb, bmb], writes=[slb])
                if lvl == 2:
                    S.op("dve", lambda: nc.vector.tensor_copy(ot[:, j * 256:j * 256 + 256], sl[:, 0:256]), reads=[slb, pab], writes=[ob])
                    continue
                S.op("act", lambda: nc.scalar.activation(out=pl[:], in_=sl[:], func=AF.Exp), reads=[slb], writes=[plb])
                S.op("act", lambda: nc.scalar.activation(out=pc[:], in_=pa[:], func=AF.Exp, scale=ATT_SCALE),
                     reads=[pab], writes=[pcb2])
                if lvl == 3:
                    S.op("dve", lambda: nc.vector.tensor_copy(ot[:, j * 256:j * 256 + 256], pl[:, 0:256]), reads=[plb, pcb2], writes=[ob])
                    continue
                for (dst, dstb, isv) in ((pd, pdb, True), (pe_, peb, False)):
                    for rg in range(4):
                        for t in range(5):
                            if t < 3:
                                rhs = pl[:, (rg * 3 + t) * 64:(rg * 3 + t + 1) * 64]
                                lhs = vt[:, j * 6 + rg + t, :] if isv else on[:]
                            else:
                                rhs = pc[:, (t - 3) * 256 + rg * 64:(t - 3) * 256 + (rg + 1) * 64]
                                lhs = vt[:, 24 + t - 3, :] if isv else on[:]
                            S.op("pe", lambda: nc.tensor.matmul(dst[:, rg * 64:(rg + 1) * 64], lhsT=lhs, rhs=rhs,
                                                                start=(t == 0), stop=(t == 4)),
                                 reads=[vb, onb, plb, pcb2], writes=[dstb])
                S.op("dve", lambda: nc.vector.reciprocal(rc[:], pe_[:, 0:256]), reads=[peb], writes=[rcb])
                S.op("dve", lambda: nc.vector.tensor_tensor(ov[:, :, 16 * j:16 * j + 16],
                                                            pd[:, 0:256].rearrange("p (r c) -> p r c", c=16),
                                                            rc[:].rearrange("p (r c) -> p r c", c=16), ALU.mult),
                     reads=[pdb, rcb], writes=[ob])
            if lvl < 5:
                S.dma("pool", aT_s[hs, :], ot[:], reads=[ob])
                continue
            pa, pab = ps[(it % 2) * 3 + 0]
            pd, pdb = ps[6]
            pe_, peb = ps[7]
            pc, pcb2 = pcs[it % 2]
            rc, rcb = rcs[it % 2]
            it += 1
            for t in range(2):
                S.op("pe", lambda: nc.tensor.matmul(pa[:, t * 64:(t + 1) * 64], lhsT=kc_[:, t * 128:(t + 1) * 128],
                                                    rhs=qt[:, 1024:OWN0], start=True, stop=True), reads=[kcb, qb], writes=[pab])
            S.op("act", lambda: nc.scalar.activation(out=pc[:, 0:128], in_=pa[:, 0:128], func=AF.Exp, scale=ATT_SCALE),
                 reads=[pab], writes=[pcb2])
            if lvl == 6:
                S.op("dve", lambda: nc.vector.tensor_copy(ot[:, 1024:OWN0], pc[:, 0:64]), reads=[pcb2], writes=[ob])
                S.dma("pool", aT_s[hs, :], ot[:], reads=[ob])
                continue
            for (dst, dstb, isv) in ((pd, pdb, True), (pe_, peb, False)):
                for t in range(2):
                    S.op("pe", lambda: nc.tensor.matmul(dst[:, 0:64], lhsT=(vt[:, 24 + t, :] if isv else on[:]),
                                                        rhs=pc[:, t * 64:(t + 1) * 64], start=(t == 0), stop=(t == 1)),
                         reads=[vb, onb, pcb2], writes=[dstb])
            S.op("dve", lambda: nc.vector.reciprocal(rc[:, 0:64], pe_[:, 0:64]), reads=[peb], writes=[rcb])
            S.op("dve", lambda: nc.vector.tensor_tensor(ot[:, 1024:OWN0], pd[:, 0:64], rc[:, 0:64], ALU.mult),
                 reads=[pdb, rcb], writes=[ob])
            S.dma("pool", aT_s[hs, :], ot[:], reads=[ob])


class ResLN:
    def __init__(self, C, M, layer, k_ln, segs_ctx, ln_d, with_router, wr_d=None, ident_d=None, own=None):
        self.C = C
        nc, S = C.nc, C.S
        self.with_router = with_router
        lv0 = layer * 6
        tmp, tmpb = C.sb("rtmp", [128, D])
        self.gt = []
        for i, cx in enumerate(segs_ctx):
            t, b = C.sb(f"gt{i}", [128, D])
            M.rep(t, b, tmp, tmpb, lv0 + (2 if k_ln == 0 else 5), cx)
            self.gt.append((t, b))
        self.lg = C.sb("lngt", [128, D])
        self.lb = C.sb("lnbt", [128, D])
        r0 = (layer * 2 + k_ln) * 2
        S.dma("sp", self.lg[0][:], ln_d[r0].partition_broadcast(128), writes=[self.lg[1]])
        S.dma("sp", self.lb[0][:], ln_d[r0 + 1].partition_broadcast(128), writes=[self.lb[1]])
        self.xt = [C.sb(f"rx{i}", [128, D]) for i in range(2)]
        self.yt = [C.sb(f"ry{i}", [128, D]) for i in range(2)]
        self.st = [C.sb(f"rs{i}", [128, 32]) for i in range(2)]
        self.k = 0
        if with_router:
            self.sct, self.sht = [], []
            for i, cx in enumerate(segs_ctx):
                t, b = C.sb(f"sc2t{i}", [128, D])
                M.rep(t, b, tmp, tmpb, lv0 + 4, cx)
                S.op("pool", lambda: nc.gpsimd.tensor_scalar(t[:], t[:], 1.0, None, ALU.add), reads=[b], writes=[b])
                self.sct.append((t, b))
                t2, b2 = C.sb(f"sh2t{i}", [128, D])
                M.rep(t2, b2, tmp, tmpb, lv0 + 3, cx)
                self.sht.append((t2, b2))
            self.wrt = C.sb("wrt", [128, KC * NE])
            S.dma("sp", self.wrt[0][:], wr_d, writes=[self.wrt[1]])
            self.idt = C.sb("idt", [128, 128])
            S.dma("sp", self.idt[0][:], ident_d, writes=[self.idt[1]])
            self.ht = [C.sb(f"rh{i}", [128, D]) for i in range(2)]
            self.hT = [C.sb(f"rhT{i}", [128, KC, 128]) for i in range(2)]
            self.af = [C.sb(f"raf{i}", [128, 2 * NE]) for i in range(2)]
            self.afT = C.sb("rafT", [NE, own])

    def emit(self, ot, ob, x_dram, t0, sz, seg, xo_dram, h2_dram, ps):
        C = self.C
        nc, S = C.nc, C.S
        k = self.k
        self.k += 1
        xt, xb = self.xt[k % 2]
        yt, yb = self.yt[k % 2]
        st, sb_ = self.st[k % 2]
        gt, gb = self.gt[seg]
        S.dma("sp", xt[0:sz, :], x_dram, writes=[xb])
        S.op("pool", lambda: nc.gpsimd.tensor_tensor(yt[0:sz, :], ot[0:sz, :], gt[0:sz, :], ALU.mult),
             reads=[ob, gb], writes=[yb])
        S.op("dve", lambda: nc.vector.scalar_tensor_tensor(yt[0:sz, :], xt[0:sz, :], DN_ALPHA, yt[0:sz, :],
                                                           ALU.mult, ALU.add), reads=[xb, yb], writes=[yb])
        for c in range(4):
            S.op("dve", lambda: nc.vector.bn_stats(st[0:sz, c * 6:(c + 1) * 6], yt[0:sz, c * 512:(c + 1) * 512]),
                 reads=[yb], writes=[sb_])
        S.op("dve", lambda: nc.vector.bn_aggr(st[0:sz, 24:26], st[0:sz, 0:24]), reads=[sb_], writes=[sb_])
        S.op("dve", lambda: nc.vector.tensor_scalar(st[0:sz, 31:32], st[0:sz, 25:26], LN_EPS, None, ALU.add),
             reads=[sb_], writes=[sb_])
        S.op("act", lambda: nc.scalar.activation(out=st[0:sz, 26:27], in_=st[0:sz, 31:32], func=AF.Sqrt),
             reads=[sb_], writes=[sb_])
        S.op("dve", lambda: nc.vector.reciprocal(st[0:sz, 26:27], st[0:sz, 26:27]), reads=[sb_], writes=[sb_])
        S.op("dve", lambda: nc.vector.tensor_scalar(yt[0:sz, :], yt[0:sz, :], st[0:sz, 24:25], st[0:sz, 26:27],
                                                    ALU.subtract, ALU.mult), reads=[yb, sb_], writes=[yb])
        S.op("pool", lambda: nc.gpsimd.tensor_tensor(yt[0:sz, :], yt[0:sz, :], self.lg[0][0:sz, :], ALU.mult),
             reads=[yb, self.lg[1]], writes=[yb])
        S.op("dve", lambda: nc.vector.tensor_tensor(yt[0:sz, :], yt[0:sz, :], self.lb[0][0:sz, :], ALU.add),
             reads=[yb, self.lb[1]], writes=[yb])
        S.dma("sp", xo_dram, yt[0:sz, :], reads=[yb])
        if not self.with_router:
            return
        ht, hb = self.ht[k % 2]
        hT, hTb = self.hT[k % 2]
        af, afb = self.af[k % 2]
        idt, idb = self.idt
        S.op("pool", lambda: nc.gpsimd.tensor_tensor(ht[0:sz, :], yt[0:sz, :], self.sct[seg][0][0:sz, :], ALU.mult),
             reads=[yb, self.sct[seg][1]], writes=[hb])
        S.op("dve", lambda: nc.vector.tensor_tensor(ht[0:sz, :], ht[0:sz, :], self.sht[seg][0][0:sz, :], ALU.add),
             reads=[hb, self.sht[seg][1]], writes=[hb])
        S.dma("sp", h2_dram, ht[0:sz, :], reads=[hb])
        for g in range(4):
            pt, pb = ps[g % 2]
            for j in range(4):
                kc = g * 4 + j
                S.op("pe", lambda: nc.tensor.transpose(out=pt[:, j * 128:j * 128 + sz], in_=ht[0:sz, kc * 128:(kc + 1) * 128],
                                                       identity=idt[0:sz, 0:sz]), reads=[hb, idb], writes=[pb])
            S.op("act", lambda: nc.scalar.copy(out=hT[:, g * 4:(g + 1) * 4, 0:sz],
                                               in_=pt[:].rearrange("p (j t) -> p j t", j=4)[:, :, 0:sz]),
                 reads=[pb], writes=[hTb])
        pt, pb = ps[2]
        wrt, wrb = self.wrt
        for kc in range(KC):
            S.op("pe", lambda: nc.tensor.matmul(pt[0:sz, 0:NE], lhsT=hT[:, kc, 0:sz], rhs=wrt[:, kc * NE:(kc + 1) * NE],
                                                start=(kc == 0), stop=(kc == KC - 1)), reads=[hTb, wrb], writes=[pb])
        S.op("dve", lambda: nc.vector.tensor_reduce(out=st[0:sz, 27:28], in_=pt[0:sz, 0:NE], axis=AX.X, op=ALU.max),
             reads=[pb], writes=[sb_])
        S.op("dve", lambda: nc.vector.tensor_scalar(st[0:sz, 28:29], st[0:sz, 27:28], -1.0, None, ALU.mult),
             reads=[sb_], writes=[sb_])
        S.op("act", lambda: nc.scalar.activation(out=af[0:sz, 0:NE], in_=pt[0:sz, 0:NE], func=AF.Exp,
                                                 bias=st[0:sz, 28:29], scale=1.0, accum_out=st[0:sz, 29:30]),
             reads=[pb, sb_], writes=[afb, sb_])
        S.op("dve", lambda: nc.vector.reciprocal(st[0:sz, 30:31], st[0:sz, 29:30]), reads=[sb_], writes=[sb_])
        S.op("dve", lambda: nc.vector.tensor_scalar(af[0:sz, NE:2 * NE], af[0:sz, 0:NE], st[0:sz, 30:31], None, ALU.mult),
             reads=[afb, sb_], writes=[afb])
        pt, pb = ps[3]
        aT, aTb = self.afT
        S.op("pe", lambda: nc.tensor.transpose(out=pt[0:NE, 0:sz], in_=af[0:sz, NE:2 * NE], identity=idt[0:sz, 0:sz]),
             reads=[afb, idb], writes=[pb])
        S.op("act", lambda: nc.scalar.copy(out=aT[:, t0:t0 + sz], in_=pt[0:NE, 0:sz]), reads=[pb], writes=[aTb])

    def flush_aff(self, affT_loc):
        own = self.afT[0].shape[1]
        self.C.S.dma("sp", affT_loc[:, 0:own], self.afT[0][:], reads=[self.afT[1]])


def emit_proj_ln(C, ps, M, layer, own, segs, aT_s, w1, w2, x_d, ln_d, wr_d, ident_d, x1_s, h2_loc, affT_loc, oscr, at_idx_d=None):
    nc, S = C.nc, C.S
    glu = w2 is not None
    nbw = 256 if glu else 512
    tiles = tok_tiles(own)
    with C.scope():
        at, ab = C.sb("at", [128, KC, own])
        if at_idx_d is None:
            S.dma("sp", at[:], aT_s.rearrange("(k p) t -> p k t", p=128), writes=[ab])
        else:
            zi, zib = C.sb("zidx", [128, KC * 4], I32)
            S.dma("sp", zi[:], at_idx_d, writes=[zib])
            zts = [C.sb(f"zt{i}", [128, 256]) for i in range(3)]
            for kc in range(KC):
                for qq in range(4):
                    zt, ztb = zts[(kc * 4 + qq) % 3]
                    S.idma(zt[:], None, aT_s, bass.IndirectOffsetOnAxis(ap=zi[:, kc * 4 + qq:kc * 4 + qq + 1], axis=0),
                           reads=[zib], writes=[ztb])
                    S.op("act", lambda: nc.scalar.copy(out=at[:, kc, qq * 256:(qq + 1) * 256], in_=zt[:]), reads=[ztb], writes=[ab])
        w1s = [C.sb(f"w1_{i}", [128, KC, nbw]) for i in range(2)]
        w2s = [C.sb(f"w2_{i}", [128, KC, nbw]) for i in range(2)] if glu else None
        stg = [C.sb(f"stg{i}", [128, nbw]) for i in range(3)]
        sgs = [C.sb(f"sgs{i}", [128, nbw]) for i in range(2)] if glu else None
        w1v = w1.rearrange("(kc p) n -> p kc n", p=128)
        w2v = w2.rearrange("(kc p) n -> p kc n", p=128) if glu else None
        pi = 0
        si = 0
        for nb in range(D // nbw):
            sl = slice(nb * nbw, (nb + 1) * nbw)
            wt, wb = w1s[nb % 2]
            S.dma("sp", wt[:], w1v[:, :, sl], writes=[wb])
            if glu:
                gt, gb = w2s[nb % 2]
                S.dma("sp", gt[:], w2v[:, :, sl], writes=[gb])
            for (t0, sz) in tiles:
                pt, pb = ps[pi % 8]
                pi += 1
                for kc in range(KC):
                    S.op("pe", lambda: nc.tensor.matmul(pt[0:sz, 0:nbw], lhsT=at[:, kc, t0:t0 + sz], rhs=wt[:, kc, :],
                                                        start=(kc == 0), stop=(kc == KC - 1)), reads=[ab, wb], writes=[pb])
                sg_t, sg_b = stg[si % 3]
                if glu:
                    qt, qb = ps[pi % 8]
                    pi += 1
                    for kc in range(KC):
                        S.op("pe", lambda: nc.tensor.matmul(qt[0:sz, 0:nbw], lhsT=at[:, kc, t0:t0 + sz], rhs=gt[:, kc, :],
                                                            start=(kc == 0), stop=(kc == KC - 1)), reads=[ab, gb], writes=[qb])
                    s2t, s2b = sgs[si % 2]
                    S.op("act", lambda: nc.scalar.activation(out=s2t[0:sz, :], in_=qt[0:sz, 0:nbw], func=AF.Sigmoid),
                         reads=[qb], writes=[s2b])
                    S.op("dve", lambda: nc.vector.tensor_tensor(sg_t[0:sz, :], pt[0:sz, 0:nbw], s2t[0:sz, :], ALU.mult),
                         reads=[pb, s2b], writes=[sg_b])
                else:
                    S.op("act", lambda: nc.scalar.copy(out=sg_t[0:sz, :], in_=pt[0:sz, 0:nbw]), reads=[pb], writes=[sg_b])
                si += 1
                S.dma("pool", oscr[t0:t0 + sz, sl], sg_t[0:sz, :], reads=[sg_b])
    with C.scope():
        R = ResLN(C, M, layer, 0, [s[2] for s in segs], ln_d, True, wr_d, ident_d, own)
        ots = [C.sb(f"ot{i}", [128, D]) for i in range(2)]
        for ti, (t0, sz) in enumerate(tiles):
            seg = [i for i, (a, b_, _) in enumerate(segs) if a <= t0 < b_][0]
            ot, ob = ots[ti % 2]
            S.dma("sp", ot[0:sz, :], oscr[t0:t0 + sz, :], writes=[ob])
            R.emit(ot, ob, x_d[t0:t0 + sz, :], t0, sz, seg, x1_s[t0:t0 + sz, :], h2_loc[t0:t0 + sz, :], ps)
        R.flush_aff(affT_loc)


def emit_moe(C, ps, own, nctx, h2_loc, affT_loc, sel_d, negm_d, offs_d, ident_d, wg, wu, wd, f_own, scr, dff=D, lvl=9, dbg=None, cc=True):
    nc, S = C.nc, C.S
    npos = 4 * own
    cap = 512
    ccap = 32 if nctx else 0
    nslot = 2 * cap + 2 * ccap
    h2_all, affT_all, part, hm_s, y_s = scr["h2_all"], scr["affT_all"], scr["part"], scr["hm_s"], scr["y_s"]
    partb = Buf("part")
    S.barrier()
    if cc:
        S.coll("AllGather", ALU.bypass, h2_loc, h2_all)
        S.coll("AllGather", ALU.bypass, affT_loc, affT_all)
        S.barrier()
    with C.scope():
        idt, idb = C.sb("midt", [128, 128])
        S.dma("sp", idt[:], ident_d, writes=[idb])
        idxT, idxTb = C.sb("idxT", [128, 16], I32)
        gT, gTb = C.sb("gT", [128, 16])
        cidx, cidxb = C.sb("cidx", [64, 2], I32)
        cg, cgb = C.sb("cg", [64, 2])
        with C.scope():
            zt, ztb = C.sb("zt", [128, 4, D])
            S.op("pool", lambda: nc.gpsimd.memset(zt[:], 0.0), writes=[ztb])
            pv = part.rearrange("(g n p) d -> g p n d", p=128, n=4)
            for g in range(8 * own // 512):
                S.dma("sp", pv[g], zt[:], reads=[ztb])
            aA, aAb = C.sb("aA", [NE, NCORES, own])
            S.dma("sp", aA[:], affT_all.rearrange("(r e) t -> e r t", e=NE), writes=[aAb])
            sel, selb = C.sb("sel", [NE, 8])
            S.dma("sp", sel[:], sel_d, writes=[selb])
            offs, offb = C.sb("offs", [4, 3])
            S.dma("sp", offs[:], offs_d, writes=[offb])
            ngm, ngb = C.sb("ngm", [4, 2, npos])
            S.dma("sp", ngm[:], negm_d, writes=[ngb])
            A, Ab = C.sb("A", [4, npos])
            AL, ALb = C.sb("AL", [4, npos])
            AC, ACb = C.sb("AC", [4, npos])
            pi = 0
            for rp in range(4):
                for (t0, tn) in tok_blocks(0, own):
                    pt, pb = ps[pi % 8]
                    pi += 1
                    S.op("pe", lambda: nc.tensor.matmul(pt[0:4, 0:tn], lhsT=sel[:, 0:4], rhs=aA[:, rp, t0:t0 + tn],
                                                        start=True, stop=False), reads=[selb, aAb], writes=[pb])
                    S.op("pe", lambda: nc.tensor.matmul(pt[0:4, 0:tn], lhsT=sel[:, 4:8], rhs=aA[:, rp + 4, t0:t0 + tn],
                                                        start=False, stop=True), reads=[selb, aAb], writes=[pb])
                    S.op("act", lambda: nc.scalar.copy(out=A[:, rp * own + t0:rp * own + t0 + tn], in_=pt[0:4, 0:tn]),
                         reads=[pb], writes=[Ab])
            S.op("dve", lambda: nc.vector.tensor_tensor(AL[:], A[:], ngm[:, 0, :], ALU.add), reads=[Ab, ngb], writes=[ALb])
            vals, vb = C.sb("vals", [4, cap])
            idxu, iub = C.sb("idxu", [4, cap], U32)
            idxf, ifb = C.sb("idxf", [4, cap])
            for it in range(cap // 8):
                s8 = slice(it * 8, it * 8 + 8)
                S.op("dve", lambda: nc.vector.max(out=vals[:, s8], in_=AL[:]), reads=[ALb], writes=[vb])
                S.op("dve", lambda: nc.vector.max_index(out=idxu[:, s8], in_max=vals[:, s8], in_values=AL[:]),
                     reads=[vb, ALb], writes=[iub])
                S.op("dve", lambda: nc.vector.match_replace(out=AL[:], in_to_replace=vals[:, s8], in_values=AL[:], imm_value=-5.0),
                     reads=[vb, ALb], writes=[ALb])
            S.op("dve", lambda: nc.vector.tensor_scalar(idxf[:], idxu[:], offs[:, 0:1], None, ALU.add),
                 reads=[iub, offb], writes=[ifb])
            pt, pb = ps[0]
            for j in range(4):
                S.op("pe", lambda: nc.tensor.transpose(out=pt[:, j * 4:j * 4 + 4], in_=idxf[0:4, j * 128:(j + 1) * 128],
                                                       identity=idt[0:4, 0:4]), reads=[ifb, idb], writes=[pb])
                S.op("pe", lambda: nc.tensor.transpose(out=pt[:, 16 + j * 4:16 + j * 4 + 4], in_=vals[0:4, j * 128:(j + 1) * 128],
                                                       identity=idt[0:4, 0:4]), reads=[vb, idb], writes=[pb])
            S.op("dve", lambda: nc.vector.tensor_copy(idxT[:], pt[:, 0:16]), reads=[pb], writes=[idxTb])
            S.op("dve", lambda: nc.vector.tensor_copy(gT[:], pt[:, 16:32]), reads=[pb], writes=[gTb])
            if nctx:
                S.op("dve", lambda: nc.vector.tensor_tensor(AC[:], A[:], ngm[:, 1, :], ALU.add), reads=[Ab, ngb], writes=[ACb])
                cv, cvb = C.sb("cv", [4, ccap])
                ciu, ciub = C.sb("ciu", [4, ccap], U32)
                cif, cifb = C.sb("cif", [4, ccap])
                Z, Zb = C.sb("Z", [4, 128])
                for it in range(ccap // 8):
                    s8 = slice(it * 8, it * 8 + 8)
                    S.op("dve", lambda: nc.vector.max(out=cv[:, s8], in_=AC[:]), reads=[ACb], writes=[cvb])
                    S.op("dve", lambda: nc.vector.max_index(out=ciu[:, s8], in_max=cv[:, s8], in_values=AC[:]),
                         reads=[cvb, ACb], writes=[ciub])
                    S.op("dve", lambda: nc.vector.match_replace(out=AC[:], in_to_replace=cv[:, s8], in_values=AC[:], imm_value=-5.0),
                         reads=[cvb, ACb], writes=[ACb])
                S.op("dve", lambda: nc.vector.tensor_scalar(cif[:], ciu[:], offs[:, 0:1], None, ALU.add),
                     reads=[ciub, offb], writes=[cifb])
                for (src_, sb2, c0) in ((cif, cifb, 0), (cv, cvb, 64)):
                    for hb in range(2):
                        S.op("dve", lambda: nc.vector.tensor_scalar(Z[:, c0 + hb * 32:c0 + hb * 32 + 32], src_[:], offs[:, 1 + hb:2 + hb],
                                                                    None, ALU.mult), reads=[sb2, offb], writes=[Zb])
                pt, pb = ps[1]
                S.op("pe", lambda: nc.tensor.transpose(out=pt[0:64, 0:4], in_=Z[0:4, 0:64], identity=idt[0:4, 0:4]),
                     reads=[Zb, idb], writes=[pb])
                S.op("pe", lambda: nc.tensor.transpose(out=pt[0:64, 4:8], in_=Z[0:4, 64:128], identity=idt[0:4, 0:4]),
                     reads=[Zb, idb], writes=[pb])
                ct, ctb = C.sb("ctmp", [64, 8])
                S.op("dve", lambda: nc.vector.tensor_copy(ct[:], pt[0:64, 0:8]), reads=[pb], writes=[ctb])
                S.op("dve", lambda: nc.vector.tensor_tensor(cidx[:], ct[:, 0:2], ct[:, 2:4], ALU.add), reads=[ctb], writes=[cidxb])
                S.op("dve", lambda: nc.vector.tensor_tensor(cg[:], ct[:, 4:6], ct[:, 6:8], ALU.add), reads=[ctb], writes=[cgb])
        if dbg is not None:
            S.dma("sp", dbg["idxT"], idxT[:], reads=[idxTb])
            S.dma("sp", dbg["gT"], gT[:], reads=[gTb])
            if nctx:
                S.dma("sp", dbg["cidx"], cidx[:], reads=[cidxb])
                S.dma("sp", dbg["cg"], cg[:], reads=[cgb])
        if lvl <= 1:
            S.barrier()
            return
        kcf = dff // 128
        tiles = [(s * 128, 128) for s in range(8)] + ([(1024, 64)] if nctx else [])

        def tile_idx(el, s):
            if s < 8:
                r = el + 2 * (s // 4)
                col = (s % 4) * 4 + r
                return idxT[:, col:col + 1], gT[:, col:col + 1], idxTb, gTb
            return cidx[:, el:el + 1], cg[:, el:el + 1], cidxb, cgb

        for el in range(2):
            with C.scope():
                xeT, xeTb = C.sb("xeT", [128, KC, nslot])
                xes = [C.sb(f"xe{i}", [128, D]) for i in range(2)]
                for s, (t0, sz) in enumerate(tiles):
                    xe, xeb = xes[s % 2]
                    ia, ga, iab, gab = tile_idx(el, s)
                    S.idma(xe[0:sz, :], None, h2_all, bass.IndirectOffsetOnAxis(ap=ia[0:sz, :], axis=0),
                           reads=[iab], writes=[xeb])
                    for g in range(4):
                        pt, pb = ps[(s * 4 + g) % 8]
                        for j in range(4):
                            kc = g * 4 + j
                            S.op("pe", lambda: nc.tensor.transpose(out=pt[:, j * 128:j * 128 + sz], in_=xe[0:sz, kc * 128:(kc + 1) * 128],
                                                                   identity=idt[0:sz, 0:sz]), reads=[xeb, idb], writes=[pb])
                        S.op("act", lambda: nc.scalar.copy(out=xeT[:, g * 4:(g + 1) * 4, t0:t0 + sz],
                                                           in_=pt[:].rearrange("p (j t) -> p j t", j=4)[:, :, 0:sz]),
                             reads=[pb], writes=[xeTb])
                if dbg is not None and el == 0:
                    S.dma("sp", dbg["xeT"], xeT[:], reads=[xeTb])
                if lvl <= 2:
                    continue
                wgs = [C.sb(f"wg{i}", [128, KC, 256]) for i in range(2)]
                wus = [C.sb(f"wu{i}", [128, KC, 256]) for i in range(2)]
                s1s = [C.sb(f"s1{i}", [128, 512]) for i in range(2)]
                hms = [C.sb(f"hm{i}", [128, nslot]) for i in range(2)]
                wgv = wg[el].rearrange("(kc p) n -> p kc n", p=128)
                wuv = wu[el].rearrange("(kc p) n -> p kc n", p=128)
                pi = 0
                k = 0
                for fb in range(dff // 256):
                    wgt, wgb = wgs[fb % 2]
                    wut, wub = wus[fb % 2]
                    S.dma("sp", wgt[:], wgv[:, :, fb * 256:(fb + 1) * 256], writes=[wgb])
                    S.dma("sp", wut[:], wuv[:, :, fb * 256:(fb + 1) * 256], writes=[wub])
                    for nn in range(2):
                        hm, hmb = hms[(fb * 2 + nn) % 2]
                        for (t0, tn) in tok_blocks(0, nslot):
                            pa, pab = ps[pi % 8]
                            pu, pub = ps[(pi + 1) % 8]
                            pi += 2
                            for kc in range(KC):
                                S.op("pe", lambda: nc.tensor.matmul(pa[:, 0:tn], lhsT=wgt[:, kc, nn * 128:(nn + 1) * 128],
                                                                    rhs=xeT[:, kc, t0:t0 + tn], start=(kc == 0), stop=(kc == KC - 1)),
                                     reads=[wgb, xeTb], writes=[pab])
                            for kc in range(KC):
                                S.op("pe", lambda: nc.tensor.matmul(pu[:, 0:tn], lhsT=wut[:, kc, nn * 128:(nn + 1) * 128],
                                                                    rhs=xeT[:, kc, t0:t0 + tn], start=(kc == 0), stop=(kc == KC - 1)),
                                     reads=[wub, xeTb], writes=[pub])
                            s1, s1b = s1s[k % 2]
                            k += 1
                            S.op("act", lambda: nc.scalar.activation(out=s1[:, 0:tn], in_=pa[:, 0:tn], func=AF.Silu),
                                 reads=[pab], writes=[s1b])
                            S.op("dve", lambda: nc.vector.tensor_tensor(hm[:, t0:t0 + tn], pu[:, 0:tn], s1[:, 0:tn], ALU.mult),
                                 reads=[s1b, pub], writes=[hmb])
                        f0 = fb * 256 + nn * 128
                        S.dma("pool", hm_s[f0:f0 + 128, 0:nslot], hm[:], reads=[hmb])
            if lvl <= 2:
                continue
            with C.scope():
                hT, hTb = C.sb("hT", [128, kcf, nslot])
                S.dma("sp", hT[:], hm_s[:, 0:nslot].rearrange("(k p) t -> p k t", p=128), writes=[hTb])
                wds = [C.sb(f"wd{i}", [128, kcf, 512]) for i in range(2)]
                yss = [C.sb(f"ys{i}", [128, 512]) for i in range(3)]
                wdv = wd[el].rearrange("(kc p) n -> p kc n", p=128)
                pi = 0
                k = 0
                for db in range(4):
                    wdt, wdb = wds[db % 2]
                    S.dma("sp", wdt[:], wdv[:, :, db * 512:(db + 1) * 512], writes=[wdb])
                    for s, (t0, sz) in enumerate(tiles):
                        ia, ga, iab, gab = tile_idx(el, s)
                        pt, pb = ps[pi % 8]
                        pi += 1
                        for kc in range(kcf):
                            S.op("pe", lambda: nc.tensor.matmul(pt[0:sz, :], lhsT=hT[:, kc, t0:t0 + sz], rhs=wdt[:, kc, :],
                                                                start=(kc == 0), stop=(kc == kcf - 1)), reads=[hTb, wdb], writes=[pb])
                        ys, ysb = yss[k % 3]
                        k += 1
                        S.op("act", lambda: nc.scalar.activation(out=ys[0:sz, :], in_=pt[0:sz, :], func=AF.Copy, scale=ga[0:sz, :]),
                             reads=[pb, gab], writes=[ysb])
                        S.dma("pool", y_s[t0:t0 + sz, db * 512:(db + 1) * 512], ys[0:sz, :], reads=[ysb])
            if lvl <= 3:
                continue
            with C.scope():
                yts = [C.sb(f"yt{i}", [128, D]) for i in range(2)]
                pgs = [C.sb(f"pg{i}", [128, D]) for i in range(2)]
                for s, (t0, sz) in enumerate(tiles):
                    ia, ga, iab, gab = tile_idx(el, s)
                    yt, ytb = yts[s % 2]
                    S.dma("sp", yt[0:sz, :], y_s[t0:t0 + sz, :], writes=[ytb])
                    if el == 1:
                        pg, pgb = pgs[s % 2]
                        S.idma(pg[0:sz, :], None, part, bass.IndirectOffsetOnAxis(ap=ia[0:sz, :], axis=0),
                               reads=[iab, partb], writes=[pgb])
                        S.op("dve", lambda: nc.vector.tensor_tensor(yt[0:sz, :], yt[0:sz, :], pg[0:sz, :], ALU.add),
                             reads=[ytb, pgb], writes=[ytb])
                    S.idma(part, bass.IndirectOffsetOnAxis(ap=ia[0:sz, :], axis=0), yt[0:sz, :], None,
                           reads=[ytb, iab], writes=[partb], owner=partb)
    S.barrier()
    if cc:
        S.coll("ReduceScatter", ALU.add, part, f_own)
        S.barrier()


def emit_comb(C, ps, M, layer, own, segs, f_own, x1_s, ln_d, x2_d):
    S = C.S
    with C.scope():
        R = ResLN(C, M, layer, 1, [s[2] for s in segs], ln_d, False)
        fts = [C.sb(f"ft{i}", [128, D]) for i in range(2)]
        for ti, (t0, sz) in enumerate(tok_tiles(own)):
            seg = [i for i, (a, b_, _) in enumerate(segs) if a <= t0 < b_][0]
            ft, fb = fts[ti % 2]
            S.dma("sp", ft[0:sz, :], f_own[t0:t0 + sz, :], writes=[fb])
            R.emit(ft, fb, x1_s[t0:t0 + sz, :], t0, sz, seg, x2_d[t0:t0 + sz, :], None, ps)


TS5 = SEQ + CTX
NCH = TS5 // 16
GELU_C = 0.7978845608028654


def emit_s5(C, ps, x2_all, m_loc, ident_d, s5p, z_loc):
    nc, S = C.nc, C.S
    with C.scope():
        idt, idb = C.sb("sidt", [128, 128])
        S.dma("sp", idt[:], ident_d, writes=[idb])
        tb, tbb = C.sb("s5tab", [128, 40, 16])
        LR, LI, DT, MAG, TH, AR, AI, NR, DEN, CR, CI, T1, T2 = [tb[:, k, :] for k in range(13)]
        p3, p3b = C.sb("s5p3", [128, 3, 17, 16])
        p2, p2b = C.sb("s5p2", [128, 3, 9, 16])
        S.dma("sp", LR, s5p["lam_re"], writes=[tbb])
        S.dma("sp", LI, s5p["lam_im"], writes=[tbb])
        S.dma("sp", DT, s5p["log_step"], writes=[tbb])

        def dv(fn):
            S.op("dve", fn, reads=[tbb, p3b, p2b], writes=[tbb, p3b, p2b])

        def ac(fn):
            S.op("act", fn, reads=[tbb], writes=[tbb])

        dv(lambda: nc.vector.tensor_scalar(LR, LR, -1e-4, None, ALU.min))
        ac(lambda: nc.scalar.activation(out=DT, in_=DT, func=AF.Exp))
        dv(lambda: nc.vector.tensor_tensor(MAG, LR, DT, ALU.mult))
        ac(lambda: nc.scalar.activation(out=MAG, in_=MAG, func=AF.Exp))
        dv(lambda: nc.vector.tensor_tensor(TH, LI, DT, ALU.mult))
        ki, kib = C.sb("s5ki", [128, 16], I32)

        def reduce_angle(dst, shift):
            dv(lambda: nc.vector.tensor_scalar(dst, TH, shift, None, ALU.add))
            S.op("dve", lambda: nc.vector.tensor_scalar(ki[:], dst, 1.0 / (2 * math.pi), None, ALU.mult), reads=[tbb], writes=[kib])
            S.op("dve", lambda: nc.vector.tensor_copy(DEN, ki[:]), reads=[kib, tbb], writes=[tbb])
            dv(lambda: nc.vector.scalar_tensor_tensor(dst, DEN, -2 * math.pi, dst, ALU.mult, ALU.add))
            dv(lambda: nc.vector.tensor_scalar(DEN, dst, math.pi, 2 * math.pi, ALU.is_gt, ALU.mult))
            dv(lambda: nc.vector.tensor_tensor(dst, dst, DEN, ALU.subtract))
            dv(lambda: nc.vector.tensor_scalar(DEN, dst, -math.pi, 2 * math.pi, ALU.is_lt, ALU.mult))
            dv(lambda: nc.vector.tensor_tensor(dst, dst, DEN, ALU.add))

        reduce_angle(T1, 0.0)
        reduce_angle(T2, 0.5 * math.pi)
        ac(lambda: nc.scalar.activation(out=AI, in_=T1, func=AF.Sin))
        ac(lambda: nc.scalar.activation(out=AR, in_=T2, func=AF.Sin))
        dv(lambda: nc.vector.tensor_tensor(AR, AR, MAG, ALU.mult))
        dv(lambda: nc.vector.tensor_tensor(AI, AI, MAG, ALU.mult))
        dv(lambda: nc.vector.tensor_scalar(NR, AR, 1.0, None, ALU.subtract))
        dv(lambda: nc.vector.tensor_tensor(DEN, LR, LR, ALU.mult))
        dv(lambda: nc.vector.tensor_tensor(T1, LI, LI, ALU.mult))
        dv(lambda: nc.vector.tensor_tensor(DEN, DEN, T1, ALU.add))
        dv(lambda: nc.vector.tensor_tensor(CR, NR, LR, ALU.mult))
        dv(lambda: nc.vector.tensor_tensor(T1, AI, LI, ALU.mult))
        dv(lambda: nc.vector.tensor_tensor(CR, CR, T1, ALU.add))
        dv(lambda: nc.vector.reciprocal(DEN, DEN))
        dv(lambda: nc.vector.tensor_tensor(CR, CR, DEN, ALU.mult))
        dv(lambda: nc.vector.tensor_tensor(CI, AI, LR, ALU.mult))
        dv(lambda: nc.vector.tensor_tensor(T1, NR, LI, ALU.mult))
        dv(lambda: nc.vector.tensor_tensor(CI, CI, T1, ALU.subtract))
        dv(lambda: nc.vector.tensor_tensor(CI, CI, DEN, ALU.mult))
        NCI = tb[:, 13, :]
        dv(lambda: nc.vector.tensor_scalar(NCI, CI, -1.0, None, ALU.mult))
        dv(lambda: nc.vector.memset(p3[:, 0, 0, :], 1.0))
        dv(lambda: nc.vector.memset(p3[:, 1, 0, :], 0.0))
        dv(lambda: nc.vector.tensor_copy(p3[:, 0, 1, :], AR))
        dv(lambda: nc.vector.tensor_copy(p3[:, 1, 1, :], AI))

        def cmul(dst_r, dst_i, ar_, ai_, br_, bi_):
            dv(lambda: nc.vector.tensor_tensor(T1, ar_, br_, ALU.mult))
            dv(lambda: nc.vector.tensor_tensor(T2, ai_, bi_, ALU.mult))
            dv(lambda: nc.vector.tensor_tensor(TH, ar_, bi_, ALU.mult))
            dv(lambda: nc.vector.tensor_tensor(DEN, ai_, br_, ALU.mult))
            dv(lambda: nc.vector.tensor_tensor(dst_r, T1, T2, ALU.subtract))
            dv(lambda: nc.vector.tensor_tensor(dst_i, TH, DEN, ALU.add))

        for j in range(2, 17):
            cmul(p3[:, 0, j, :], p3[:, 1, j, :], p3[:, 0, j - 1, :], p3[:, 1, j - 1, :], AR, AI)
        dv(lambda: nc.vector.tensor_copy(p2[:, 0, 0, :], p3[:, 0, 16, :]))
        dv(lambda: nc.vector.tensor_copy(p2[:, 1, 0, :], p3[:, 1, 16, :]))
        for k in range(1, 9):
            cmul(p2[:, 0, k, :], p2[:, 1, k, :], p2[:, 0, k - 1, :], p2[:, 1, k - 1, :], p2[:, 0, k - 1, :], p2[:, 1, k - 1, :])
        dv(lambda: nc.vector.tensor_scalar(p3[:, 2, :, :], p3[:, 1, :, :], -1.0, None, ALU.mult))
        dv(lambda: nc.vector.tensor_scalar(p2[:, 2, :, :], p2[:, 1, :, :], -1.0, None, ALU.mult))
        mv, mvb = C.sb("s5mv", [128, 16])
        mr = m_loc.rearrange("r (v h p) -> r v p h", h=2, p=128)
        for r in range(3):
            S.dma("sp", mv[:, r * 4:r * 4 + 2], mr[r, 7], writes=[mvb], allow_slow_non_contiguous=True)
            S.dma("sp", mv[:, r * 4 + 2:r * 4 + 4], mr[r, 6], writes=[mvb], allow_slow_non_contiguous=True)
            S.op("dve", lambda: nc.vector.tensor_scalar(mv[:, r * 4:r * 4 + 2], mv[:, r * 4:r * 4 + 2], 1.0, None, ALU.add),
                 reads=[mvb], writes=[mvb])
        S.dma("sp", mv[:, 12:14], s5p["dskip"], writes=[mvb])
        sidx, sidxb = C.sb("s5idx", [128, 72], I32)
        S.dma("sp", sidx[:], s5p["xidx"], writes=[sidxb])
        bre, breb = C.sb("s5bre", [128, 128])
        bim, bimb = C.sb("s5bim", [128, 128])
        bd = [C.sb(f"s5bd{i}", [128, 128]) for i in range(2)]
        lb = [C.sb(f"s5lb{i}", [128, 128]) for i in range(2)]
        lc = [C.sb(f"s5lc{i}", [128, 128]) for i in range(2)]
        core0 = ps
        for b in range(B):
            with C.scope():
                ul, ulb = C.sb("s5ul", [128, 2, TS5])
                ya, yab = C.sb("s5ya", [128, 2, SEQ])
                xt_, xtb = C.sb("s5xt", [128, 256])
                S.op("pool", lambda: nc.gpsimd.memset(ya[:], 0.0), writes=[yab])
                k = 0
                xts = [C.sb(f"s5x{i}", [128, 256]) for i in range(3)]
                for q in range(4):
                    r0 = (4 * b + q) * OWN0
                    for (t0, sz) in tok_tiles(OWN0):
                        xt2, xt2b = xts[k % 3]
                        ic = (4 * b + q) * 9 + t0 // 128
                        S.idma(xt2[0:sz, :], None, x2_all.rearrange("t (j c) -> (t j) c", c=256),
                               bass.IndirectOffsetOnAxis(ap=sidx[0:sz, ic:ic + 1], axis=0), reads=[sidxb], writes=[xt2b])
                        pt, pb = ps[k % 8]
                        k += 1
                        isctx = t0 >= 1024
                        dst0 = (q * 64) if isctx else (CTX + q * 1024 + t0)
                        rr = 2 if isctx else b
                        for h in range(2):
                            S.op("pe", lambda: nc.tensor.transpose(out=pt[:, h * 128:h * 128 + sz], in_=xt2[0:sz, h * 128:(h + 1) * 128],
                                                                   identity=idt[0:sz, 0:sz]), reads=[xt2b, idb], writes=[pb])
                        for h in range(2):
                            S.op("act", lambda: nc.scalar.activation(out=ul[:, h, dst0:dst0 + sz], in_=pt[:, h * 128:h * 128 + sz],
                                                                     func=AF.Identity, scale=mv[:, rr * 4 + h:rr * 4 + h + 1],
                                                                     bias=mv[:, rr * 4 + 2 + h:rr * 4 + 3 + h]),
                                 reads=[pb, mvb], writes=[ulb])
                xr = [C.sb(f"s5xr{i}", [128, TS5]) for i in range(2)]
                xi = [C.sb(f"s5xi{i}", [128, TS5]) for i in range(2)]
                er = [C.sb(f"s5er{i}", [128, NCH]) for i in range(2)]
                ei = [C.sb(f"s5ei{i}", [128, NCH]) for i in range(2)]
                for d in range(2):
                    rev = d == 1
                    for gp in range(8):
                        col = d * 8 + gp
                        ch = gp // 4
                        S.dma("sp", bre[:], s5p["b_re"][d, gp], writes=[breb])
                        S.dma("sp", bim[:], s5p["b_im"][d, gp], writes=[bimb])
                        for ri in range(2):
                            t_, tb_ = bd[ri]
                            a0, a0b = (bre, breb) if ri == 0 else (bim, bimb)
                            a1, a1b = (bim, bimb) if ri == 0 else (bre, breb)
                            sc1_ = tb[:, 13, col:col + 1] if ri == 0 else tb[:, 10, col:col + 1]
                            S.op("dve", lambda: nc.vector.tensor_scalar(t_[:], a0[:], tb[:, 9, col:col + 1], None, ALU.mult),
                                 reads=[a0b, tbb], writes=[tb_])
                            S.op("dve", lambda: nc.vector.scalar_tensor_tensor(t_[:], a1[:], sc1_, t_[:], ALU.mult, ALU.add),
                                 reads=[a1b, tbb, tb_], writes=[tb_])
                            pt, pb = ps[ri]
                            S.op("pe", lambda: nc.tensor.transpose(out=pt[:, 0:128], in_=t_[:], identity=idt[:]),
                                 reads=[tb_, idb], writes=[pb])
                            S.op("act", lambda: nc.scalar.copy(out=lb[ri][0][:], in_=pt[:, 0:128]), reads=[pb], writes=[lb[ri][1]])
                        S.dma("sp", lc[0][0][:], s5p["c_re"][d, gp], writes=[lc[0][1]])
                        S.dma("sp", lc[1][0][:], s5p["c_im"][d, gp], writes=[lc[1][1]])
                        S.op("pool", lambda: nc.gpsimd.tensor_scalar(lc[1][0][:], lc[1][0][:], -1.0, None, ALU.mult),
                             reads=[lc[1][1]], writes=[lc[1][1]])
                        X = [xr[0], xi[0]]
                        pi = 2
                        segs = [(0, CTX, 0), (CTX, TS5, CTX)] if not rev else [(CTX, TS5, 0), (0, CTX, SEQ)]
                        for (u0, u1, p0) in segs:
                            for (t0, tn) in tok_blocks(u0, u1):
                                for ri in range(2):
                                    pt, pb = ps[pi % 8]
                                    pi += 1
                                    S.op("pe", lambda: nc.tensor.matmul(pt[:, 0:tn], lhsT=lb[ri][0][:], rhs=ul[:, ch, t0:t0 + tn],
                                                                        start=True, stop=True), reads=[lb[ri][1], ulb], writes=[pb])
                                    dst = p0 + (t0 - u0)
                                    S.op("act", lambda: nc.scalar.copy(out=X[ri][0][:, dst:dst + tn], in_=pt[:, 0:tn]),
                                         reads=[pb], writes=[X[ri][1]])
                        cur = 0
                        for k in range(4):
                            s = 1 << k
                            (ir, irb), (ii, iib) = xr[cur], xi[cur]
                            (orr, orb), (oi, oib) = xr[1 - cur], xi[1 - cur]
                            v = lambda t: t[:].rearrange("p (c j) -> p c j", j=16)
                            if not rev:
                                dsl, ssl, csl = slice(s, 16), slice(0, 16 - s), slice(0, s)
                            else:
                                dsl, ssl, csl = slice(0, 16 - s), slice(s, 16), slice(16 - s, 16)
                            Pr, Pi_, NPi = p3[:, 0, s, col:col + 1], p3[:, 1, s, col:col + 1], p3[:, 2, s, col:col + 1]
                            S.op("dve", lambda: nc.vector.scalar_tensor_tensor(v(orr)[:, :, dsl], v(ir)[:, :, ssl], Pr, v(ir)[:, :, dsl], ALU.mult, ALU.add),
                                 reads=[irb, p3b], writes=[orb])
                            S.op("dve", lambda: nc.vector.scalar_tensor_tensor(v(orr)[:, :, dsl], v(ii)[:, :, ssl], NPi, v(orr)[:, :, dsl], ALU.mult, ALU.add),
                                 reads=[iib, orb, p3b], writes=[orb])
                            S.op("dve", lambda: nc.vector.scalar_tensor_tensor(v(oi)[:, :, dsl], v(ii)[:, :, ssl], Pr, v(ii)[:, :, dsl], ALU.mult, ALU.add),
                                 reads=[iib, p3b], writes=[oib])
                            S.op("dve", lambda: nc.vector.scalar_tensor_tensor(v(oi)[:, :, dsl], v(ir)[:, :, ssl], Pi_, v(oi)[:, :, dsl], ALU.mult, ALU.add),
                                 reads=[irb, oib, p3b], writes=[oib])
                            S.op("act", lambda: nc.scalar.copy(out=v(orr)[:, :, csl], in_=v(ir)[:, :, csl]), reads=[irb], writes=[orb])
                            S.op("pool", lambda: nc.gpsimd.tensor_copy(v(oi)[:, :, csl], v(ii)[:, :, csl]), reads=[iib], writes=[oib])
                            cur = 1 - cur
                        (x_r, x_rb), (x_i, x_ib) = xr[cur], xi[cur]
                        vr = x_r[:].rearrange("p (c j) -> p c j", j=16)
                        vi = x_i[:].rearrange("p (c j) -> p c j", j=16)
                        je = 0 if rev else 15
                        S.op("act", lambda: nc.scalar.copy(out=er[0][0][:], in_=vr[:, :, je]), reads=[x_rb], writes=[er[0][1]])
                        S.op("act", lambda: nc.scalar.copy(out=ei[0][0][:], in_=vi[:, :, je]), reads=[x_ib], writes=[ei[0][1]])
                        ce = 0
                        for k in range(9):
                            s = 1 << k
                            (ir, irb), (ii, iib) = er[ce], ei[ce]
                            (orr, orb), (oi, oib) = er[1 - ce], ei[1 - ce]
                            if not rev:
                                dsl, ssl, csl = slice(s, NCH), slice(0, NCH - s), slice(0, s)
                            else:
                                dsl, ssl, csl = slice(0, NCH - s), slice(s, NCH), slice(NCH - s, NCH)
                            Pr, Pi_, NPi = p2[:, 0, k, col:col + 1], p2[:, 1, k, col:col + 1], p2[:, 2, k, col:col + 1]
                            S.op("dve", lambda: nc.vector.scalar_tensor_tensor(orr[:, dsl], ir[:, ssl], Pr, ir[:, dsl], ALU.mult, ALU.add),
                                 reads=[irb, p2b], writes=[orb])
                            S.op("dve", lambda: nc.vector.scalar_tensor_tensor(orr[:, dsl], ii[:, ssl], NPi, orr[:, dsl], ALU.mult, ALU.add),
                                 reads=[iib, orb, p2b], writes=[orb])
                            S.op("dve", lambda: nc.vector.scalar_tensor_tensor(oi[:, dsl], ii[:, ssl], Pr, ii[:, dsl], ALU.mult, ALU.add),
                                 reads=[iib, p2b], writes=[oib])
                            S.op("dve", lambda: nc.vector.scalar_tensor_tensor(oi[:, dsl], ir[:, ssl], Pi_, oi[:, dsl], ALU.mult, ALU.add),
                                 reads=[irb, oib, p2b], writes=[oib])
                            S.op("act", lambda: nc.scalar.copy(out=orr[:, csl], in_=ir[:, csl]), reads=[irb], writes=[orb])
                            S.op("act", lambda: nc.scalar.copy(out=oi[:, csl], in_=ii[:, csl]), reads=[iib], writes=[oib])
                            ce = 1 - ce
                        (E_r, E_rb), (E_i, E_ib) = er[ce], ei[ce]
                        for j in range(16):
                            pw = (16 - j) if rev else (j + 1)
                            Pr, Pi_, NPi = p3[:, 0, pw, col:col + 1], p3[:, 1, pw, col:col + 1], p3[:, 2, pw, col:col + 1]
                            if not rev:
                                xs_, es_ = slice(1, NCH), slice(0, NCH - 1)
                            else:
                                xs_, es_ = slice(0, NCH - 1), slice(1, NCH)
                            S.op("dve", lambda: nc.vector.scalar_tensor_tensor(vr[:, xs_, j], E_r[:, es_], Pr, vr[:, xs_, j], ALU.mult, ALU.add),
                                 reads=[E_rb, x_rb, p3b], writes=[x_rb])
                            S.op("dve", lambda: nc.vector.scalar_tensor_tensor(vr[:, xs_, j], E_i[:, es_], NPi, vr[:, xs_, j], ALU.mult, ALU.add),
                                 reads=[E_ib, x_rb, p3b], writes=[x_rb])
                            S.op("dve", lambda: nc.vector.scalar_tensor_tensor(vi[:, xs_, j], E_i[:, es_], Pr, vi[:, xs_, j], ALU.mult, ALU.add),
                                 reads=[E_ib, x_ib, p3b], writes=[x_ib])
                            S.op("dve", lambda: nc.vector.scalar_tensor_tensor(vi[:, xs_, j], E_r[:, es_], Pi_, vi[:, xs_, j], ALU.mult, ALU.add),
                                 reads=[E_rb, x_ib, p3b], writes=[x_ib])
                        l0 = 0 if rev else CTX
                        for (t0, tn) in tok_blocks(0, SEQ):
                            pt, pb = ps[pi % 8]
                            pi += 1
                            S.op("pe", lambda: nc.tensor.matmul(pt[:, 0:tn], lhsT=lc[0][0][:], rhs=x_r[:, l0 + t0:l0 + t0 + tn],
                                                                start=True, stop=False), reads=[lc[0][1], x_rb], writes=[pb])
                            S.op("pe", lambda: nc.tensor.matmul(pt[:, 0:tn], lhsT=lc[1][0][:], rhs=x_i[:, l0 + t0:l0 + t0 + tn],
                                                                start=False, stop=True), reads=[lc[1][1], x_ib], writes=[pb])
                            S.op("dve", lambda: nc.vector.tensor_tensor(ya[:, ch, t0:t0 + tn], pt[:, 0:tn], ya[:, ch, t0:t0 + tn], ALU.add),
                                 reads=[pb, yab], writes=[yab])
                for h in range(2):
                    yv = ya[:, h, :]
                    uv = ul[:, h, CTX:TS5]
                    w1, w1b = xr[0]
                    S.op("dve", lambda: nc.vector.scalar_tensor_tensor(yv, uv, mv[:, 12 + h:13 + h], yv, ALU.mult, ALU.add),
                         reads=[ulb, mvb, yab], writes=[yab])
                    S.op("act", lambda: nc.scalar.activation(out=w1[:, 0:SEQ], in_=yv, func=AF.Square), reads=[yab], writes=[w1b])
                    S.op("dve", lambda: nc.vector.tensor_scalar(w1[:, 0:SEQ], w1[:, 0:SEQ], 0.044715, 1.0, ALU.mult, ALU.add),
                         reads=[w1b], writes=[w1b])
                    S.op("dve", lambda: nc.vector.tensor_tensor(w1[:, 0:SEQ], w1[:, 0:SEQ], yv, ALU.mult), reads=[w1b, yab], writes=[w1b])
                    S.op("act", lambda: nc.scalar.activation(out=w1[:, 0:SEQ], in_=w1[:, 0:SEQ], func=AF.Tanh, scale=GELU_C),
                         reads=[w1b], writes=[w1b])
                    S.op("dve", lambda: nc.vector.tensor_scalar(w1[:, 0:SEQ], w1[:, 0:SEQ], 0.5, 0.5, ALU.mult, ALU.add),
                         reads=[w1b], writes=[w1b])
                    S.op("dve", lambda: nc.vector.tensor_tensor(w1[:, 0:SEQ], w1[:, 0:SEQ], yv, ALU.mult), reads=[w1b, yab], writes=[w1b])
                    S.dma("sp", z_loc[h * 128:(h + 1) * 128, b * SEQ:(b + 1) * SEQ], w1[:, 0:SEQ], reads=[w1b])


def fm(x2d):
    T = x2d.shape[0]
    return np.ascontiguousarray(x2d.T.reshape(KC, 128, T).transpose(1, 0, 2).reshape(128, KC * T))


def fm_vec(vs):
    a = np.stack(vs, 1)
    n = a.shape[1]
    return np.ascontiguousarray(a.reshape(KC, 128, n).transpose(1, 0, 2).reshape(128, KC * n))


def natten_tables(rpb, q):
    klr = np.arange(4)[:, None, None, None]
    w = np.arange(32)[None, :, None, None]
    qlr = np.arange(4)[None, None, :, None]
    qc = np.arange(16)[None, None, None, :]
    bias = np.zeros((NH, 3, 3, 128, 64), np.float32)
    mask = np.zeros((3, 3, 3, 128, 64), np.float32)
    colok_j = {}
    for j in range(4):
        kcol = COL_CS[j] + w
        qcol = 16 * j + qc
        cstart = np.clip(qcol - 8, 0, 48)
        colok_j[j] = ((kcol >= cstart) & (kcol < cstart + 16), np.clip(kcol - qcol + 15, 0, 30))
    assert np.array_equal(colok_j[1][0], colok_j[2][0]) and np.array_equal(colok_j[1][1], colok_j[2][1])
    for jc, j in enumerate((0, 1, 3)):
        colok, dc = colok_j[j]
        for kt in range(3):
            dr = np.clip(4 * kt + klr - qlr + 3, 0, 14)
            drb, dcb = np.broadcast_arrays(dr, dc)
            bias[:, jc, kt] = rpb[:, drb, dcb].reshape(NH, 128, 64)
            for rs, q0 in enumerate((16 * q, 16 * q + 4 if q < 3 else 52, 16 * q + 12)):
                r = q0 + qlr
                keyrow = q0 - 4 + 4 * kt + klr
                rs_ = np.clip(r - 4, 0, 56)
                ok = (keyrow >= 0) & (keyrow < 64) & (keyrow >= rs_) & (keyrow < rs_ + 8) & colok
                mask[rs, jc, kt] = np.where(np.broadcast_to(ok, (4, 32, 4, 16)), 0.0, -30000.0).reshape(128, 64)
    bias = np.ascontiguousarray(bias.transpose(0, 3, 1, 2, 4).reshape(NH, 128, 576))
    mask = np.ascontiguousarray(mask.transpose(3, 0, 1, 2, 4).reshape(128, 27 * 64))
    return bias, mask


def host_inputs(inp, upto=99):
    x, ctx = inp["x"], inp["ctx"]
    c3 = np.concatenate([inp["c"], inp["c_ctx"][None]], 0).astype(np.float32)
    cT = np.ascontiguousarray(c3.T.reshape(KC, 128, 3).transpose(1, 0, 2).reshape(128, KC * 3))
    ln = np.stack([inp["ln_g"], inp["ln_b"]], 2).reshape(8, D).astype(np.float32)
    ident = np.eye(128, dtype=np.float32)
    ones = np.ones((128, 128), np.float32)
    maps = []
    for core in range(NCORES):
        b, q = divmod(core, 4)
        d = {}
        wm = np.empty((D, MODN), np.float32)
        bm = np.empty((3, MODN), np.float32)
        for l in range(DEPTH):
            for v in range(6):
                lv = l * 6 + v
                cs = slice(v * D + core * 256, v * D + core * 256 + 256)
                wm[:, lv * 256:(lv + 1) * 256] = inp["w_mod"][l][:, cs]
                bm[:, lv * 256:(lv + 1) * 256] = inp["b_mod"][l][None, cs]
        d["cT"], d["wmod"], d["bmod"] = cT, wm, bm
        bs = np.zeros((128, 2), np.float32)
        bs[:, b] = 1.0
        d["bsel"] = bs
        band = np.zeros((24, 64, D), np.float32)
        r0 = 16 * q - 4
        lo, hi = max(r0, 0), min(r0 + 24, 64)
        band[lo - r0:hi - r0] = x[b].reshape(64, 64, D)[lo:hi]
        cown = ctx[b, 64 * q:64 * q + 64]
        coth = np.concatenate([ctx[b, :64 * q], ctx[b, 64 * q + 64:]], 0)
        d["xT"] = fm(np.concatenate([band.reshape(1536, D), cown, coth], 0))
        d["x_own"] = np.ascontiguousarray(np.concatenate([x[b, 1024 * q:1024 * q + 1024], cown], 0))
        d["wqkv"] = np.ascontiguousarray(inp["na_w_qkv"][0][256 * core:256 * core + 256])
        d["wo"] = np.ascontiguousarray(inp["na_w_o"][0][256 * core:256 * core + 256])
        d["bias_tab"], d["mask_tab"] = natten_tables(inp["na_rpb"][0], q)
        d["ln"] = ln
        d["wr"] = np.stack([fm_vec(list(inp["moe_w_router"][l].T)) for l in range(DEPTH)])
        d["ident"], d["ones"] = ident, ones
        sel = np.zeros((NE, 8), np.float32)
        for el in range(2):
            sel[2 * core + el, el] = 1.0
            sel[2 * core + el, 4 + 2 + el] = 1.0
        d["sel"] = sel
        npos = 4 * OWN0
        d["offs0"] = np.array([[0, 1, 0], [0, 1, 0], [npos, 0, 1], [npos, 0, 1]], np.float32)
        isctx = (np.arange(npos) % OWN0) >= 1024
        ng = np.zeros((4, 2, npos), np.float32)
        ng[:, 0, isctx] = -2.0
        ng[:, 1, ~isctx] = -2.0
        d["negm0"] = ng
        for l in range(DEPTH):
            e0 = 2 * core
            d[f"wg{l}"] = np.ascontiguousarray(inp["moe_w_gate"][l][e0:e0 + 2])
            d[f"wu{l}"] = np.ascontiguousarray(inp["moe_w_up"][l][e0:e0 + 2])
            d[f"wd{l}"] = np.ascontiguousarray(inp["moe_w_down"][l][e0:e0 + 2])
        d.update(s5_host(inp, core))
        d["wval"] = np.ascontiguousarray(inp["s5_w_val"][0][256 * core:256 * core + 256])
        d["wgate"] = np.ascontiguousarray(inp["s5_w_gate"][0][256 * core:256 * core + 256])
        maps.append(d)
    return maps


def s5_host(inp, core):
    d = {}
    g0 = 16 * core
    lam_re = np.zeros((128, 16), np.float32)
    lam_im = np.zeros((128, 16), np.float32)
    lstep = np.zeros((128, 16), np.float32)
    bre = np.zeros((2, 8, 128, 128), np.float32)
    bim = np.zeros((2, 8, 128, 128), np.float32)
    cre = np.zeros((2, 8, 128, 128), np.float32)
    cim = np.zeros((2, 8, 128, 128), np.float32)
    for dd in range(2):
        for gp in range(8):
            for g2 in range(2):
                g = g0 + 2 * gp + g2
                rows = slice(g2 * 64, g2 * 64 + 64)
                col = dd * 8 + gp
                lam_re[rows, col] = inp["s5_lam_re"][0, dd, g]
                lam_im[rows, col] = inp["s5_lam_im"][0, dd, g]
                lstep[rows, col] = inp["s5_log_step"][0, dd, g]
                c0 = 32 * (gp % 4) + g2 * 16
                bre[dd, gp, rows, c0:c0 + 16] = inp["s5_b_re"][0, dd, g]
                bim[dd, gp, rows, c0:c0 + 16] = inp["s5_b_im"][0, dd, g]
                cre[dd, gp, rows, c0:c0 + 16] = inp["s5_c_re"][0, dd, g].T
                cim[dd, gp, rows, c0:c0 + 16] = inp["s5_c_im"][0, dd, g].T
    d["s5_lam_re"], d["s5_lam_im"], d["s5_log_step"] = lam_re, lam_im, lstep
    d["s5_b_re"], d["s5_b_im"], d["s5_c_re"], d["s5_c_im"] = bre, bim, cre, cim
    d["s5_dskip"] = np.ascontiguousarray(inp["s5_d"][0][256 * core:256 * core + 256].reshape(2, 128).T)
    xidx = np.zeros((128, 72), np.int32)
    p = np.arange(128)
    for r in range(8):
        for ti in range(9):
            xidx[:, r * 9 + ti] = np.minimum(r * OWN0 + ti * 128 + p, NCORES * OWN0 - 1) * 8 + core
    d["s5_xidx"] = xidx
    chan = np.arange(KC)[None, :, None] * 128 + p[:, None, None]
    d["zidx"] = ((chan * 8 + core) * 4 + np.arange(4)[None, None, :]).reshape(128, KC * 4).astype(np.int32)
    return d


def build_full(stop=99):
    C = Ctx()
    nc, S = C.nc, C.S
    ps = C.psum_banks(8)
    dbg = Buf("dbg")

    def dump(name, src, shape):
        o = C.dram_out(name, shape)
        S.dma("sp", o, src, owner=dbg)

    def gather_weight(name, rows_per, cols):
        ext = C.dram_in(name, [rows_per, cols])
        loc = C.dram_tmp(name + "_loc", [rows_per, cols])
        full = C.dram_tmp(name + "_all", [rows_per * NCORES, cols])
        S.dma("sp", loc, ext, owner=dbg)
        return loc, full

    cT = C.dram_in("cT", [128, KC * 3])
    wmod = C.dram_in("wmod", [D, MODN])
    bmod = C.dram_in("bmod", [3, MODN])
    bsel = C.dram_in("bsel", [128, 2])
    m_loc = C.dram_tmp("m_loc", [3, MODN])
    m_all = C.dram_tmp("m_all", [3 * NCORES, MODN])
    bst, bsb = C.sb("bsel_t", [128, 2])
    S.dma("sp", bst[:], bsel, writes=[bsb])
    M = Mod(C, m_all, bst, bsb)

    wq_loc, wq_all = gather_weight("wqkv", 256, 3 * D)
    wo_loc, wo_all = gather_weight("wo", 256, D)
    if stop > 5:
        wv_loc, wv_all = gather_weight("wval", 256, D)
        wgt_loc, wgt_all = gather_weight("wgate", 256, D)
    S.barrier()
    S.coll("AllGather", ALU.bypass, wq_loc, wq_all)
    S.coll("AllGather", ALU.bypass, wo_loc, wo_all)
    if stop > 5:
        S.coll("AllGather", ALU.bypass, wv_loc, wv_all)
        S.coll("AllGather", ALU.bypass, wgt_loc, wgt_all)

    emit_mod(C, ps, cT, wmod, bmod, m_loc, m_all)
    S.barrier()
    if stop == 0:
        dump("d_m", m_all, [3 * NCORES, MODN])
        with C.scope():
            t, b = C.sb("dt", [128, D])
            tmp, tmpb = C.sb("dtmp", [128, D])
            M.rep(t, b, tmp, tmpb, 2, False)
            o = C.dram_out("d_rep", [128, D])
            S.dma("sp", o, t[:], reads=[b])
            t2, b2 = C.sb("dt2", [128, 2 * KC])
            M.fm(t2, b2, 0, 4, False)
            o2 = C.dram_out("d_fm", [128, KC])
            S.dma("sp", o2, t2[:, 0:KC], reads=[b2])
        return C.close()

    xT = C.dram_in("xT", [128, KC * NQKV])
    qT_s = C.dram_tmp("qT_s", [D, OWN0])
    kT_s = C.dram_tmp("kT_s", [D, NQKV])
    v_s = C.dram_tmp("v_s", [NQKV, D])
    emit_qkv(C, ps, M, xT, wq_all, qT_s, kT_s, v_s)
    if stop == 1:
        dump("d_q", qT_s, [D, OWN0])
        dump("d_k", kT_s, [D, NQKV])
        dump("d_v", v_s, [NQKV, D])
        return C.close()

    bias_tab = C.dram_in("bias_tab", [NH, 128, 576])
    mask_tab = C.dram_in("mask_tab", [128, 27 * 64])
    ones_d = C.dram_in("ones", [128, 128])
    aT_s = C.dram_tmp("aT_s", [D, OWN0])
    emit_attn(C, ps, qT_s, kT_s, v_s, bias_tab, mask_tab, ones_d, aT_s)
    if stop == 2:
        dump("d_a", aT_s, [D, OWN0])
        return C.close()

    x_own = C.dram_in("x_own", [OWN0, D])
    ln_d = C.dram_in("ln", [8, D])
    wr_d = C.dram_in("wr", [DEPTH, 128, KC * NE])
    ident_d = C.dram_in("ident", [128, 128])
    oscr = C.dram_tmp("oscr", [OWN0, D])
    x1_s = C.dram_tmp("x1_s", [OWN0, D])
    h2_loc = C.dram_tmp("h2_loc", [OWN0, D])
    affT_loc = C.dram_tmp("affT_loc", [NE, OWN0])
    segs0 = [(0, 1024, False), (1024, OWN0, True)]
    emit_proj_ln(C, ps, M, 0, OWN0, segs0, aT_s, wo_all, None, x_own, ln_d, wr_d[0], ident_d, x1_s, h2_loc, affT_loc, oscr)
    if stop == 3:
        dump("d_x1", x1_s, [OWN0, D])
        dump("d_h2", h2_loc, [OWN0, D])
        dump("d_aff", affT_loc, [NE, OWN0])
        return C.close()

    sel_d = C.dram_in("sel", [NE, 8])
    offs0 = C.dram_in("offs0", [4, 3])
    negm0 = C.dram_in("negm0", [4, 2, 4 * OWN0])
    wg0 = C.dram_in("wg0", [2, D, D])
    wu0 = C.dram_in("wu0", [2, D, D])
    wd0 = C.dram_in("wd0", [2, D, D])
    scr0 = {"h2_all": C.dram_tmp("h2_all0", [NCORES * OWN0, D]), "affT_all": C.dram_tmp("affT_all0", [NCORES * NE, OWN0]),
            "part": C.dram_tmp("part0", [NCORES * OWN0, D]), "hm_s": C.dram_tmp("hm_s", [D, OWN0]),
            "y_s": C.dram_tmp("y_s", [OWN0, D])}
    f_own0 = C.dram_tmp("f_own0", [OWN0, D])
    emit_moe(C, ps, OWN0, 64, h2_loc, affT_loc, sel_d, negm0, offs0, ident_d, wg0, wu0, wd0, f_own0, scr0)
    if stop == 4:
        dump("d_f", f_own0, [OWN0, D])
        dump("d_x1", x1_s, [OWN0, D])
        dump("d_h2", h2_loc, [OWN0, D])
        dump("d_aff", affT_loc, [NE, OWN0])
        return C.close()
    x2_loc = C.dram_tmp("x2_loc", [OWN0, D])
    emit_comb(C, ps, M, 0, OWN0, segs0, f_own0, x1_s, ln_d, x2_loc)
    if stop == 50:
        dump("x2_o", x2_loc, [OWN0, D])
        dump("m_all_o", m_all, [3 * NCORES, MODN])
        dump("m_loc_o", m_loc, [3, MODN])
        return C.close()
    if stop == 5:
        dump("d_x2", x2_loc, [OWN0, D])
        dump("d_f", f_own0, [OWN0, D])
        dump("d_x1", x1_s, [OWN0, D])
        dump("d_h2", h2_loc, [OWN0, D])
        dump("d_aff", affT_loc, [NE, OWN0])
        return C.close()

    x2_all = C.dram_tmp("x2_all", [NCORES * OWN0, D])
    S.barrier()
    S.coll("AllGather", ALU.bypass, x2_loc, x2_all)
    S.barrier()
    s5p = s5_inputs(C)
    z_loc = C.dram_tmp("z_loc", [256, B * SEQ])
    z_all = C.dram_tmp("z_all", [D, B * SEQ])
    emit_s5(C, ps, x2_all, m_loc, ident_d, s5p, z_loc)
    S.barrier()
    S.coll("AllGather", ALU.bypass, z_loc, z_all)
    S.barrier()
    if stop == 6:
        dump("d_z", z_loc, [256, B * SEQ])
        return C.close()
    zidx_d = C.dram_in("zidx", [128, KC * 4], I32)
    oscr1 = C.dram_tmp("oscr1", [OWN1, D])
    x1_s1 = C.dram_tmp("x1_s1", [OWN1, D])
    h2_loc1 = C.dram_tmp("h2_loc1", [OWN0, D])
    affT_loc1 = C.dram_tmp("affT_loc1", [NE, OWN0])
    with C.scope():
        z0, z0b = C.sb("z0", [64, D])
        S.op("pool", lambda: nc.gpsimd.memset(z0[:], 0.0), writes=[z0b])
        S.dma("sp", h2_loc1[OWN1:OWN0, :], z0[:], reads=[z0b])
        S.dma("sp", affT_loc1[:, OWN1:OWN0], z0[0:NE, 0:OWN0 - OWN1], reads=[z0b])
    segs1 = [(0, OWN1, False)]
    emit_proj_ln(C, ps, M, 1, OWN1, segs1, z_all.rearrange("ch (r t) -> (ch r) t", t=256), wv_all, wgt_all, x2_loc[0:OWN1, :],
                 ln_d, wr_d[1], ident_d, x1_s1, h2_loc1, affT_loc1, oscr1, at_idx_d=zidx_d)
    if stop == 7:
        dump("d_x1", x1_s1, [OWN1, D])
        dump("d_aff", affT_loc1, [NE, OWN0])
        return C.close()
    wg1 = C.dram_in("wg1", [2, D, D])
    wu1 = C.dram_in("wu1", [2, D, D])
    wd1 = C.dram_in("wd1", [2, D, D])
    f_own1 = C.dram_tmp("f_own1", [OWN0, D])
    emit_moe(C, ps, OWN0, 64, h2_loc1, affT_loc1, sel_d, negm0, offs0, ident_d, wg1, wu1, wd1, f_own1, scr0)
    y_out = C.dram_out("y", [OWN1, D])
    emit_comb(C, ps, M, 1, OWN1, segs1, f_own1, x1_s1, ln_d, y_out)
    return C.close()


def build_attn_test(nh, lvl=9):
    C = Ctx()
    ps = C.psum_banks(8)
    qT_s = C.dram_in("qT", [nh * 128, OWN0])
    kT_s = C.dram_in("kT", [nh * 128, NQKV])
    v_s = C.dram_in("v", [NQKV, nh * 128])
    bias_tab = C.dram_in("bias_tab", [nh, 128, 576])
    mask_tab = C.dram_in("mask_tab", [128, 27 * 64])
    ones_d = C.dram_in("ones", [128, 128])
    aT = C.dram_out("aT", [nh * 128, OWN0])
    emit_attn(C, ps, qT_s, kT_s, v_s, bias_tab, mask_tab, ones_d, aT, nh=nh, lvl=lvl)
    return C.close()


def build_moe_test(own, nctx, dff, lvl):
    C = Ctx()
    S = C.S
    ps = C.psum_banks(8)
    npos = 4 * own
    nslot = 1024 + (64 if nctx else 0)
    h2e = C.dram_in("h2", [own, D])
    afe = C.dram_in("affT", [NE, own])
    h2_loc = C.dram_tmp("h2_loc", [own, D])
    affT_loc = C.dram_tmp("affT_loc", [NE, own])
    dd = Buf("dd")
    S.dma("sp", h2_loc, h2e, owner=dd)
    S.dma("sp", affT_loc, afe, owner=dd)
    sel_d = C.dram_in("sel", [NE, 8])
    offs = C.dram_in("offs", [4, 3])
    negm = C.dram_in("negm", [4, 2, npos])
    ident_d = C.dram_in("ident", [128, 128])
    wg = C.dram_in("wg", [2, D, dff])
    wu = C.dram_in("wu", [2, D, dff])
    wd = C.dram_in("wd", [2, dff, D])
    scr = {"h2_all": C.dram_tmp("h2_all", [NCORES * own, D]), "affT_all": C.dram_tmp("affT_all", [NCORES * NE, own]),
           "part": C.dram_tmp("part", [NCORES * own, D]), "hm_s": C.dram_tmp("hm_s", [dff, nslot]),
           "y_s": C.dram_tmp("y_s", [nslot, D])}
    f_own = C.dram_tmp("f_own", [own, D])
    dbg = {"idxT": C.dram_out("o_idxT", [128, 16], I32), "gT": C.dram_out("o_gT", [128, 16]),
           "cidx": C.dram_out("o_cidx", [64, 2], I32), "cg": C.dram_out("o_cg", [64, 2]),
           "xeT": C.dram_out("o_xeT", [128, KC, nslot])}
    emit_moe(C, ps, own, nctx, h2_loc, affT_loc, sel_d, negm, offs, ident_d, wg, wu, wd, f_own, scr, dff=dff, lvl=lvl, dbg=dbg)
    fo = C.dram_out("o_f", [own, D])
    S.dma("sp", fo, f_own, owner=dd)
    return C.close()


def build_ffn_test(dff, variant):
    C = Ctx()
    nc, S = C.nc, C.S
    ps = C.psum_banks(8)
    nslot = 1088
    kcf = dff // 128
    xe_d = C.dram_in("xeT", [128, KC, nslot])
    g_d = C.dram_in("g", [128, 16])
    wg = C.dram_in("wg", [D, dff])
    wu = C.dram_in("wu", [D, dff])
    wd = C.dram_in("wd", [dff, D])
    hm_s = C.dram_tmp("hm_s", [dff, nslot])
    y_o = C.dram_out("y", [nslot, D])
    tiles = [(s * 128, 128) for s in range(8)] + [(1024, 64)]
    gT, gTb = C.sb("gT", [128, 16])
    S.dma("sp", gT[:], g_d, writes=[gTb])
    with C.scope():
        xeT, xeTb = C.sb("xeT", [128, KC, nslot])
        S.dma("sp", xeT[:], xe_d, writes=[xeTb])
        wgs = [C.sb(f"wg{i}", [128, KC, 256]) for i in range(2)]
        wus = [C.sb(f"wu{i}", [128, KC, 256]) for i in range(2)]
        s1s = [C.sb(f"s1{i}", [128, 512]) for i in range(2)]
        hms = [C.sb(f"hm{i}", [128, nslot]) for i in range(2)]
        wgv = wg.rearrange("(kc p) n -> p kc n", p=128)
        wuv = wu.rearrange("(kc p) n -> p kc n", p=128)
        pi = 0
        k = 0
        for fb in range(dff // 256):
            wgt, wgb = wgs[fb % 2]
            wut, wub = wus[fb % 2]
            S.dma("sp", wgt[:], wgv[:, :, fb * 256:(fb + 1) * 256], writes=[wgb])
            S.dma("sp", wut[:], wuv[:, :, fb * 256:(fb + 1) * 256], writes=[wub])
            for nn in range(2):
                hm, hmb = hms[(fb * 2 + nn) % 2]
                for (t0, tn) in tok_blocks(0, nslot):
                    pa, pab = ps[pi % 8]
                    pu, pub = ps[(pi + 1) % 8]
                    pi += 2
                    for kc in range(KC):
                        S.op("pe", lambda: nc.tensor.matmul(pa[:, 0:tn], lhsT=wgt[:, kc, nn * 128:(nn + 1) * 128],
                                                            rhs=xeT[:, kc, t0:t0 + tn], start=(kc == 0), stop=(kc == KC - 1)),
                             reads=[wgb, xeTb], writes=[pab])
                    for kc in range(KC):
                        S.op("pe", lambda: nc.tensor.matmul(pu[:, 0:tn], lhsT=wut[:, kc, nn * 128:(nn + 1) * 128],
                                                            rhs=xeT[:, kc, t0:t0 + tn], start=(kc == 0), stop=(kc == KC - 1)),
                             reads=[wub, xeTb], writes=[pub])
                    s1, s1b = s1s[k % 2]
                    k += 1
                    if variant == 0:
                        S.op("act", lambda: nc.scalar.activation(out=s1[:, 0:tn], in_=pa[:, 0:tn], func=AF.Silu),
                             reads=[pab], writes=[s1b])
                    else:
                        S.op("act", lambda: nc.scalar.activation(out=s1[:, 0:tn], in_=pa[:, 0:tn], func=AF.Sigmoid),
                             reads=[pab], writes=[s1b])
                        S.op("dve", lambda: nc.vector.tensor_tensor(s1[:, 0:tn], pa[:, 0:tn], s1[:, 0:tn], ALU.mult),
                             reads=[pab, s1b], writes=[s1b])
                    S.op("dve", lambda: nc.vector.tensor_tensor(hm[:, t0:t0 + tn], pu[:, 0:tn], s1[:, 0:tn], ALU.mult),
                         reads=[s1b, pub], writes=[hmb])
                f0 = fb * 256 + nn * 128
                S.dma("pool", hm_s[f0:f0 + 128, 0:nslot], hm[:], reads=[hmb])
    with C.scope():
        hT, hTb = C.sb("hT", [128, kcf, nslot])
        S.dma("sp", hT[:], hm_s[:, 0:nslot].rearrange("(k p) t -> p k t", p=128), writes=[hTb])
        wds = [C.sb(f"wd{i}", [128, kcf, 512]) for i in range(2)]
        yss = [C.sb(f"ys{i}", [128, 512]) for i in range(3)]
        wdv = wd.rearrange("(kc p) n -> p kc n", p=128)
        pi = 0
        k = 0
        for db in range(4):
            wdt, wdb = wds[db % 2]
            S.dma("sp", wdt[:], wdv[:, :, db * 512:(db + 1) * 512], writes=[wdb])
            for s, (t0, sz) in enumerate(tiles):
                ga = gT[:, s:s + 1]
                pt, pb = ps[pi % 8]
                pi += 1
                for kc in range(kcf):
                    S.op("pe", lambda: nc.tensor.matmul(pt[0:sz, :], lhsT=hT[:, kc, t0:t0 + sz], rhs=wdt[:, kc, :],
                                                        start=(kc == 0), stop=(kc == kcf - 1)), reads=[hTb, wdb], writes=[pb])
                ys, ysb = yss[k % 3]
                k += 1
                S.op("act", lambda: nc.scalar.activation(out=ys[0:sz, :], in_=pt[0:sz, :], func=AF.Copy, scale=ga[0:sz, :]),
                     reads=[pb, gTb], writes=[ysb])
                S.dma("pool", y_o[t0:t0 + sz, db * 512:(db + 1) * 512], ys[0:sz, :], reads=[ysb])
    return C.close()


S5_KEYS = ("lam_re", "lam_im", "log_step", "b_re", "b_im", "c_re", "c_im", "dskip", "xidx")


def s5_inputs(C):
    shp = {"lam_re": [128, 16], "lam_im": [128, 16], "log_step": [128, 16], "b_re": [2, 8, 128, 128], "b_im": [2, 8, 128, 128],
           "c_re": [2, 8, 128, 128], "c_im": [2, 8, 128, 128], "dskip": [128, 2]}
    s5p = {k: C.dram_in("s5_" + k, v) for k, v in shp.items()}
    s5p["xidx"] = C.dram_in("s5_xidx", [128, 72], I32)
    return s5p


def build_s5_test():
    C = Ctx()
    S = C.S
    ps = C.psum_banks(8)
    x2e = C.dram_in("x2", [OWN0, D])
    mle = C.dram_in("mloc", [3, MODN])
    ident_d = C.dram_in("ident", [128, 128])
    x2_loc = C.dram_tmp("x2_loc", [OWN0, D])
    m_loc = C.dram_tmp("m_loc", [3, MODN])
    x2_all = C.dram_tmp("x2_all", [NCORES * OWN0, D])
    dd = Buf("dd")
    S.dma("sp", x2_loc, x2e, owner=dd)
    S.dma("sp", m_loc, mle, owner=dd)
    S.barrier()
    S.coll("AllGather", ALU.bypass, x2_loc, x2_all)
    S.barrier()
    s5p = s5_inputs(C)
    z_loc = C.dram_out("z", [256, B * SEQ])
    emit_s5(C, ps, x2_all, m_loc, ident_d, s5p, z_loc)
    return C.close()


def build_l1():
    C = Ctx()
    nc, S = C.nc, C.S
    ps = C.psum_banks(8)
    dd = Buf("dd")
    x2e = C.dram_in("x2", [OWN0, D])
    m_e = C.dram_in("m_all", [3 * NCORES, MODN])
    ml_e = C.dram_in("m_loc", [3, MODN])
    x2_loc = C.dram_tmp("x2_loc", [OWN0, D])
    m_all = C.dram_tmp("m_all_i", [3 * NCORES, MODN])
    m_loc = C.dram_tmp("m_loc_i", [3, MODN])
    S.dma("sp", x2_loc, x2e, owner=dd)
    S.dma("sp", m_all, m_e, owner=dd)
    S.dma("sp", m_loc, ml_e, owner=dd)
    bsel = C.dram_in("bsel", [128, 2])
    bst, bsb = C.sb("bsel_t", [128, 2])
    S.dma("sp", bst[:], bsel, writes=[bsb])
    M = Mod(C, m_all, bst, bsb)

    def gather_weight(name, rows_per, cols):
        ext = C.dram_in(name, [rows_per, cols])
        loc = C.dram_tmp(name + "_loc", [rows_per, cols])
        full = C.dram_tmp(name + "_all", [rows_per * NCORES, cols])
        S.dma("sp", loc, ext, owner=dd)
        return loc, full

    wv_loc, wv_all = gather_weight("wval", 256, D)
    wgt_loc, wgt_all = gather_weight("wgate", 256, D)
    x2_all = C.dram_tmp("x2_all", [NCORES * OWN0, D])
    S.barrier()
    S.coll("AllGather", ALU.bypass, x2_loc, x2_all)
    S.barrier()
    ident_d = C.dram_in("ident", [128, 128])
    s5p = s5_inputs(C)
    z_loc = C.dram_tmp("z_loc", [256, B * SEQ])
    z_all = C.dram_tmp("z_all", [D, B * SEQ])
    emit_s5(C, ps, x2_all, m_loc, ident_d, s5p, z_loc)
    S.barrier()
    S.coll("AllGather", ALU.bypass, z_loc, z_all)
    S.coll("AllGather", ALU.bypass, wv_loc, wv_all)
    S.coll("AllGather", ALU.bypass, wgt_loc, wgt_all)
    S.barrier()
    ln_d = C.dram_in("ln", [8, D])
    wr_d = C.dram_in("wr", [DEPTH, 128, KC * NE])
    zidx_d = C.dram_in("zidx", [128, KC * 4], I32)
    oscr1 = C.dram_tmp("oscr1", [OWN1, D])
    x1_s1 = C.dram_tmp("x1_s1", [OWN1, D])
    h2_loc1 = C.dram_tmp("h2_loc1", [OWN0, D])
    affT_loc1 = C.dram_tmp("affT_loc1", [NE, OWN0])
    with C.scope():
        z0, z0b = C.sb("z0", [64, D])
        S.op("pool", lambda: nc.gpsimd.memset(z0[:], 0.0), writes=[z0b])
        S.dma("sp", h2_loc1[OWN1:OWN0, :], z0[:], reads=[z0b])
        S.dma("sp", affT_loc1[:, OWN1:OWN0], z0[0:NE, 0:OWN0 - OWN1], reads=[z0b])
    segs1 = [(0, OWN1, False)]
    emit_proj_ln(C, ps, M, 1, OWN1, segs1, z_all.rearrange("ch (r t) -> (ch r) t", t=256), wv_all, wgt_all, x2_loc[0:OWN1, :],
                 ln_d, wr_d[1], ident_d, x1_s1, h2_loc1, affT_loc1, oscr1, at_idx_d=zidx_d)
    sel_d = C.dram_in("sel", [NE, 8])
    offs0 = C.dram_in("offs0", [4, 3])
    negm0 = C.dram_in("negm0", [4, 2, 4 * OWN0])
    wg1 = C.dram_in("wg1", [2, D, D])
    wu1 = C.dram_in("wu1", [2, D, D])
    wd1 = C.dram_in("wd1", [2, D, D])
    scr = {"h2_all": C.dram_tmp("h2_all0", [NCORES * OWN0, D]), "affT_all": C.dram_tmp("affT_all0", [NCORES * NE, OWN0]),
           "part": C.dram_tmp("part0", [NCORES * OWN0, D]), "hm_s": C.dram_tmp("hm_s", [D, OWN0]),
           "y_s": C.dram_tmp("y_s", [OWN0, D])}
    f_own1 = C.dram_tmp("f_own1", [OWN0, D])
    emit_moe(C, ps, OWN0, 64, h2_loc1, affT_loc1, sel_d, negm0, offs0, ident_d, wg1, wu1, wd1, f_own1, scr)
    y_out = C.dram_out("y", [OWN1, D])
    emit_comb(C, ps, M, 1, OWN1, segs1, f_own1, x1_s1, ln_d, y_out)
    return C.close()


L0_KEYS = ("cT", "wmod", "bmod", "bsel", "wqkv", "wo", "xT", "bias_tab", "mask_tab", "ones", "x_own", "ln", "wr", "ident",
           "sel", "offs0", "negm0", "wg0", "wu0", "wd0")
L1_KEYS = ("bsel", "wval", "wgate", "ident", "s5_lam_re", "s5_lam_im", "s5_log_step", "s5_b_re", "s5_b_im", "s5_c_re", "s5_c_im",
           "s5_dskip", "s5_xidx", "ln", "wr", "zidx", "sel", "offs0", "negm0", "wg1", "wu1", "wd1")


def _mod_from_ext(C, S, dd):
    m_e = C.dram_in("m_all", [3 * NCORES, MODN])
    m_all = C.dram_tmp("m_all_i", [3 * NCORES, MODN])
    S.dma("sp", m_all, m_e, owner=dd)
    bsel = C.dram_in("bsel", [128, 2])
    bst, bsb = C.sb("bsel_t", [128, 2])
    S.dma("sp", bst[:], bsel, writes=[bsb])
    return Mod(C, m_all, bst, bsb)


def build_p1():
    C = Ctx()
    ps = C.psum_banks(8)
    cT = C.dram_in("cT", [128, KC * 3])
    wmod = C.dram_in("wmod", [D, MODN])
    bmod = C.dram_in("bmod", [3, MODN])
    m_loc = C.dram_out("m_loc_o", [3, MODN])
    emit_mod(C, ps, cT, wmod, bmod, m_loc, None, cc=False)
    return C.close()


def build_p2():
    C = Ctx()
    S = C.S
    ps = C.psum_banks(8)
    dd = Buf("dd")
    M = _mod_from_ext(C, S, dd)
    wq = C.dram_in("wqkv_full", [D, 3 * D])
    wo = C.dram_in("wo_full", [D, D])
    xT = C.dram_in("xT", [128, KC * NQKV])
    qT_s = C.dram_tmp("qT_s", [D, OWN0])
    kT_s = C.dram_tmp("kT_s", [D, NQKV])
    v_s = C.dram_tmp("v_s", [NQKV, D])
    S.barrier()
    emit_qkv(C, ps, M, xT, wq, qT_s, kT_s, v_s)
    bias_tab = C.dram_in("bias_tab", [NH, 128, 576])
    mask_tab = C.dram_in("mask_tab", [128, 27 * 64])
    ones_d = C.dram_in("ones", [128, 128])
    aT_s = C.dram_tmp("aT_s", [D, OWN0])
    emit_attn(C, ps, qT_s, kT_s, v_s, bias_tab, mask_tab, ones_d, aT_s)
    x_own = C.dram_in("x_own", [OWN0, D])
    ln_d = C.dram_in("ln", [8, D])
    wr_d = C.dram_in("wr", [DEPTH, 128, KC * NE])
    ident_d = C.dram_in("ident", [128, 128])
    oscr = C.dram_tmp("oscr", [OWN0, D])
    x1 = C.dram_out("x1_o", [OWN0, D])
    h2 = C.dram_out("h2_o", [OWN0, D])
    aff = C.dram_out("affT_o", [NE, OWN0])
    emit_proj_ln(C, ps, M, 0, OWN0, [(0, 1024, False), (1024, OWN0, True)], aT_s, wo, None, x_own, ln_d, wr_d[0], ident_d,
                 x1, h2, aff, oscr)
    return C.close()


def build_p3():
    C = Ctx()
    S = C.S
    ps = C.psum_banks(8)
    dd = Buf("dd")
    h2e = C.dram_in("h2_all", [NCORES * OWN0, D])
    afe = C.dram_in("affT_all", [NCORES * NE, OWN0])
    scr = {"h2_all": C.dram_tmp("h2_all_i", [NCORES * OWN0, D]), "affT_all": C.dram_tmp("affT_all_i", [NCORES * NE, OWN0]),
           "part": C.dram_tmp("part_i", [NCORES * OWN0, D]), "hm_s": C.dram_tmp("hm_s", [D, OWN0]),
           "y_s": C.dram_tmp("y_s", [OWN0, D])}
    S.dma("sp", scr["h2_all"], h2e, owner=dd)
    S.dma("sp", scr["affT_all"], afe, owner=dd)
    sel_d = C.dram_in("sel", [NE, 8])
    offs0 = C.dram_in("offs0", [4, 3])
    negm0 = C.dram_in("negm0", [4, 2, 4 * OWN0])
    ident_d = C.dram_in("ident", [128, 128])
    wg = C.dram_in("wg", [2, D, D])
    wu = C.dram_in("wu", [2, D, D])
    wd = C.dram_in("wd", [2, D, D])
    emit_moe(C, ps, OWN0, 64, None, None, sel_d, negm0, offs0, ident_d, wg, wu, wd, None, scr, cc=False)
    po = C.dram_out("part_o", [NCORES * OWN0, D])
    S.dma("sp", po, scr["part"], owner=dd)
    return C.close()


def build_p4(layer, own):
    C = Ctx()
    nc, S = C.nc, C.S
    ps = C.psum_banks(8)
    dd = Buf("dd")
    M = _mod_from_ext(C, S, dd)
    parts = C.dram_in("parts", [NCORES, own, D])
    x1 = C.dram_in("x1", [own, D])
    ln_d = C.dram_in("ln", [8, D])
    x2 = C.dram_out("x2_o", [own, D])
    segs = [(0, 1024, False)] + ([(1024, own, True)] if own > 1024 else [])
    S.barrier()
    with C.scope():
        R = ResLN(C, M, layer, 1, [s[2] for s in segs], ln_d, False)
        accs = [C.sb(f"acc{i}", [128, D]) for i in range(2)]
        pbs = [C.sb(f"pb{i}", [128, D]) for i in range(2)]
        k = 0
        for ti, (t0, sz) in enumerate(tok_tiles(own)):
            seg = [i for i, (a, b_, _) in enumerate(segs) if a <= t0 < b_][0]
            acc, accb = accs[ti % 2]
            S.dma("sp", acc[0:sz, :], parts[0, t0:t0 + sz, :], writes=[accb])
            for r in range(1, NCORES):
                pt_, ptb_ = pbs[k % 2]
                k += 1
                S.dma("sp", pt_[0:sz, :], parts[r, t0:t0 + sz, :], writes=[ptb_])
                eng = "dve" if r % 2 else "pool"
                e_ = nc.vector if r % 2 else nc.gpsimd
                S.op(eng, lambda: e_.tensor_tensor(acc[0:sz, :], acc[0:sz, :], pt_[0:sz, :], ALU.add),
                     reads=[accb, ptb_], writes=[accb])
            R.emit(acc, accb, x1[t0:t0 + sz, :], t0, sz, seg, x2[t0:t0 + sz, :], None, ps)
    return C.close()


def build_p5():
    C = Ctx()
    S = C.S
    ps = C.psum_banks(8)
    x2c = C.dram_in("x2_ch", [NCORES * OWN0, 256])
    m_loc = C.dram_in("m_loc", [3, MODN])
    ident_d = C.dram_in("ident", [128, 128])
    s5p = s5_inputs(C)
    z = C.dram_out("z_o", [256, B * SEQ])
    emit_s5(C, ps, x2c, m_loc, ident_d, s5p, z)
    return C.close()


def build_p6():
    C = Ctx()
    S = C.S
    ps = C.psum_banks(8)
    dd = Buf("dd")
    M = _mod_from_ext(C, S, dd)
    zT = C.dram_in("zT_own", [D, OWN1])
    w1 = C.dram_in("wval_full", [D, D])
    w2 = C.dram_in("wgate_full", [D, D])
    x = C.dram_in("x2_own", [OWN1, D])
    ln_d = C.dram_in("ln", [8, D])
    wr_d = C.dram_in("wr", [DEPTH, 128, KC * NE])
    ident_d = C.dram_in("ident", [128, 128])
    x1 = C.dram_out("x1_o", [OWN1, D])
    h2 = C.dram_out("h2_o", [OWN1, D])
    aff = C.dram_out("affT_o", [NE, OWN1])
    oscr = C.dram_tmp("oscr", [OWN1, D])
    S.barrier()
    emit_proj_ln(C, ps, M, 1, OWN1, [(0, OWN1, False)], zT, w1, w2, x, ln_d, wr_d[1], ident_d, x1, h2, aff, oscr)
    return C.close()


def kernel_nocc(inp, maps):
    pr = _NC_CACHE
    if "p1" not in pr:
        pr.update(p1=build_p1(), p2=build_p2(), p3=build_p3(), p40=build_p4(0, OWN0), p5=build_p5(), p6=build_p6(),
                  p41=build_p4(1, OWN1))
    r = run_spmd(pr["p1"], [{k: m[k] for k in ("cT", "wmod", "bmod")} for m in maps])
    m_loc = [r[c]["m_loc_o"] for c in range(NCORES)]
    m_all = np.ascontiguousarray(np.concatenate(m_loc, 0))
    wq = np.ascontiguousarray(inp["na_w_qkv"][0])
    wo = np.ascontiguousarray(inp["na_w_o"][0])
    base = ("bsel", "ln", "wr", "ident")
    r = run_spmd(pr["p2"], [dict({k: m[k] for k in base + ("xT", "bias_tab", "mask_tab", "ones", "x_own")},
                                 m_all=m_all, wqkv_full=wq, wo_full=wo) for m in maps])
    x1 = [r[c]["x1_o"] for c in range(NCORES)]

    def moe(h2_list, aff_list, l):
        h2_all = np.ascontiguousarray(np.concatenate(h2_list, 0))
        aff_all = np.ascontiguousarray(np.concatenate(aff_list, 0))
        rr = run_spmd(pr["p3"], [dict({k: m[k] for k in ("sel", "offs0", "negm0", "ident")}, h2_all=h2_all, affT_all=aff_all,
                                      wg=m[f"wg{l}"], wu=m[f"wu{l}"], wd=m[f"wd{l}"]) for m in maps])
        return [rr[c]["part_o"] for c in range(NCORES)]

    parts = moe([r[c]["h2_o"] for c in range(NCORES)], [r[c]["affT_o"] for c in range(NCORES)], 0)
    r = run_spmd(pr["p40"], [dict({k: maps[c][k] for k in ("bsel", "ln")}, m_all=m_all, x1=x1[c],
                                  parts=np.ascontiguousarray(np.stack([p[c * OWN0:(c + 1) * OWN0] for p in parts])))
                             for c in range(NCORES)])
    x2 = [r[c]["x2_o"] for c in range(NCORES)]
    x2_all = np.concatenate(x2, 0)
    p_ = np.arange(128)
    ins5 = []
    for c in range(NCORES):
        d = {k: maps[c][k] for k in ("ident", "s5_lam_re", "s5_lam_im", "s5_log_step", "s5_b_re", "s5_b_im", "s5_c_re", "s5_c_im",
                                     "s5_dskip")}
        xi = np.zeros((128, 72), np.int32)
        for rk in range(8):
            for ti in range(9):
                xi[:, rk * 9 + ti] = np.minimum(rk * OWN0 + ti * 128 + p_, NCORES * OWN0 - 1)
        d["s5_xidx"] = xi
        d["x2_ch"] = np.ascontiguousarray(x2_all[:, 256 * c:256 * c + 256])
        d["m_loc"] = m_loc[c]
        ins5.append(d)
    r = run_spmd(pr["p5"], ins5)
    zT = np.concatenate([r[c]["z_o"] for c in range(NCORES)], 0)
    wv = np.ascontiguousarray(inp["s5_w_val"][0])
    wg_ = np.ascontiguousarray(inp["s5_w_gate"][0])
    r = run_spmd(pr["p6"], [dict({k: maps[c][k] for k in base}, m_all=m_all, wval_full=wv, wgate_full=wg_,
                                 zT_own=np.ascontiguousarray(zT[:, c * OWN1:(c + 1) * OWN1]),
                                 x2_own=np.ascontiguousarray(x2[c][:OWN1])) for c in range(NCORES)])
    x1b = [r[c]["x1_o"] for c in range(NCORES)]
    pad = lambda a, ax: np.ascontiguousarray(np.concatenate([a, np.zeros_like(a[:64] if ax == 0 else a[:, :64])], ax))
    parts = moe([pad(r[c]["h2_o"], 0) for c in range(NCORES)], [pad(r[c]["affT_o"], 1) for c in range(NCORES)], 1)
    r = run_spmd(pr["p41"], [dict({k: maps[c][k] for k in ("bsel", "ln")}, m_all=m_all, x1=x1b[c],
                                  parts=np.ascontiguousarray(np.stack([p[c * OWN0:c * OWN0 + OWN1] for p in parts])))
                             for c in range(NCORES)])
    out = np.empty((B, SEQ, D), np.float32)
    for core in range(NCORES):
        b, q = divmod(core, 4)
        out[b, q * OWN1:(q + 1) * OWN1] = r[core]["x2_o"]
    return out


_NC_CACHE = {}


USE_COLLECTIVES = False


def kernel(**inputs):
    inp = {k: np.asarray(v) for k, v in inputs.items()}
    maps = host_inputs(inp)
    if not USE_COLLECTIVES:
        return kernel_nocc(inp, maps)
    if "full" not in _NC_CACHE:
        _NC_CACHE["full"] = build_full()
    res = run_spmd(_NC_CACHE["full"], maps)
    out = np.empty((B, SEQ, D), np.float32)
    for core in range(NCORES):
        b, q = divmod(core, 4)
        out[b, q * OWN1:(q + 1) * OWN1] = res[core]["y"]
    return out
```
